# Optimizing a Trainium2 kernel written in Bass

```python
import math
import jax, jax.numpy as jnp
from jax import lax
import numpy as np

D_MODEL = 1024
BATCH = 8
SEQ = 4096
DEPTH = 1

HEAD_DIM = 64
MIX_WIDTH = D_MODEL
ATTN_WIDTH = 3 * MIX_WIDTH // 4
N_ATTN_HEADS = ATTN_WIDTH // HEAD_DIM
CONV_WIDTH = MIX_WIDTH - ATTN_WIDTH
CONV_KERNEL = 31
DILATION_PATTERNS = ((128, 1), (512, 4), (2048, 16))
ROPE_DIM = HEAD_DIM // 4
ROPE_THETA = 500000.0
D_FF = 2816
FFN_CONV_KERNEL = 3
IN_PROJ_WIDTH = 3 * ATTN_WIDTH + 2 * CONV_WIDTH
DEEPNORM_ALPHA = (2.0 * DEPTH) ** 0.25
DEEPNORM_BETA = (8.0 * DEPTH) ** -0.25
LN_EPS = 1e-5

kernel_name = "hybrid_dilated_attn_conformer_convffn_deepnorm"


def layer_norm(x, g, b):
    xf = x.astype(jnp.float32)
    mu = jnp.mean(xf, axis=-1, keepdims=True)
    var = jnp.mean(jnp.square(xf - mu), axis=-1, keepdims=True)
    return ((xf - mu) * lax.rsqrt(var + LN_EPS) * g.astype(jnp.float32) + b.astype(jnp.float32)).astype(x.dtype)


def depthwise_conv(x, w, b):
    width = w.shape[0]
    pad = (width - 1) // 2
    y = lax.conv_general_dilated(
        x, w[:, None, :].astype(x.dtype), window_strides=(1,),
        padding=[(pad, width - 1 - pad)],
        dimension_numbers=("NWC", "WIO", "NWC"),
        feature_group_count=x.shape[-1])
    return y + b.astype(x.dtype)


def apply_partial_rope(t, positions):
    half = ROPE_DIM // 2
    inv_freq = ROPE_THETA ** (-jnp.arange(half, dtype=jnp.float32) / half)
    ang = positions.astype(jnp.float32)[:, None, :, None] * inv_freq
    cos, sin = jnp.cos(ang), jnp.sin(ang)
    tf = t.astype(jnp.float32)
    x1, x2 = tf[..., :half], tf[..., half:ROPE_DIM]
    out = jnp.concatenate([x1 * cos - x2 * sin, x2 * cos + x1 * sin, tf[..., ROPE_DIM:]], axis=-1)
    return out.astype(t.dtype)


def dilated_band_attention(q, k, v, dilation, radius):
    B, H, S, E = q.shape
    L = S // dilation
    blk = radius
    nblk = -(-L // blk)
    Lp = nblk * blk

    def to_residue(t):
        return t.reshape(B, H, L, dilation, E).transpose(0, 1, 3, 2, 4)

    qr, kr, vr = to_residue(q), to_residue(k), to_residue(v)
    qb = jnp.pad(qr, ((0, 0),) * 3 + ((0, Lp - L), (0, 0))).reshape(B, H, dilation, nblk, blk, E)

    def key_windows(t):
        tp = jnp.pad(t, ((0, 0),) * 3 + ((blk, Lp - L + blk), (0, 0)))
        tp = tp.reshape(B, H, dilation, nblk + 2, blk, E)
        return jnp.concatenate([tp[:, :, :, :-2], tp[:, :, :, 1:-1], tp[:, :, :, 2:]], axis=4)

    kw, vw = key_windows(kr), key_windows(vr)
    scores = jnp.einsum("bhrnqe,bhrnke->bhrnqk", qb.astype(jnp.float32), kw.astype(jnp.float32)) * (E ** -0.5)

    q_idx = jnp.arange(nblk)[:, None, None] * blk + jnp.arange(blk)[None, :, None]
    k_idx = (jnp.arange(nblk)[:, None, None] - 1) * blk + jnp.arange(3 * blk)[None, None, :]
    rel = k_idx - q_idx
    valid = ((jnp.abs(rel) <= radius) & (k_idx >= 0) & (k_idx < L)) | (rel == 0)
    scores = jnp.where(valid, scores, -jnp.inf)

    lse = jax.nn.logsumexp(scores, axis=-1)
    probs = jnp.exp(scores - lse[..., None])
    o = jnp.einsum("bhrnqk,bhrnke->bhrnqe", probs, vw.astype(jnp.float32))
    o = o.reshape(B, H, dilation, Lp, E)[:, :, :, :L].transpose(0, 1, 3, 2, 4).reshape(B, H, S, E)
    lse = lse.reshape(B, H, dilation, Lp)[..., :L].transpose(0, 1, 3, 2).reshape(B, H, S)
    return o, lse


def hybrid_mixer(x, positions, w_in, b_glu, conv_w, conv_b, conv_ln_g, conv_ln_b, w_out):
    B, S, _ = x.shape
    proj = x @ w_in
    q, k, v, c_val, c_gate = jnp.split(
        proj, [ATTN_WIDTH, 2 * ATTN_WIDTH, 3 * ATTN_WIDTH, 3 * ATTN_WIDTH + CONV_WIDTH], axis=-1)

    def heads(t):
        return t.reshape(B, S, N_ATTN_HEADS, HEAD_DIM).transpose(0, 2, 1, 3)

    q = apply_partial_rope(heads(q), positions)
    k = apply_partial_rope(heads(k), positions)
    v = heads(v)

    outs, lses = [], []
    for window, dilation in DILATION_PATTERNS:
        o, lse = dilated_band_attention(q, k, v, dilation, window // (2 * dilation))
        outs.append(o)
        lses.append(lse)
    weights = jax.nn.softmax(jnp.stack(lses, axis=0), axis=0)
    attn = jnp.einsum("pbhs,pbhse->bshe", weights, jnp.stack(outs, axis=0))
    attn = attn.reshape(B, S, ATTN_WIDTH).astype(x.dtype)

    u = (c_val + b_glu[:CONV_WIDTH]) * jax.nn.sigmoid(c_gate + b_glu[CONV_WIDTH:])
    u = depthwise_conv(u, conv_w, conv_b)
    u = jax.nn.silu(layer_norm(u, conv_ln_g, conv_ln_b))

    mixed = jnp.concatenate([attn, u.astype(x.dtype)], axis=-1)
    return mixed @ w_out


def conv_gated_mlp(x, w_ffn_in, ffn_conv_w, ffn_conv_b, w_ffn_out):
    h = x @ w_ffn_in
    gate, up = jnp.split(h, [D_FF], axis=-1)
    gate = depthwise_conv(gate, ffn_conv_w, ffn_conv_b)
    return (jax.nn.silu(gate) * up) @ w_ffn_out


def setup_inputs(seed: int = 0) -> dict:
    key = jax.random.key(seed)
    ks = jax.random.split(key, 20)
    f32 = jnp.float32

    def nrm(k, shape, scale):
        return jax.random.normal(k, shape, f32) * scale

    x = jax.random.normal(ks[0], (BATCH, SEQ, D_MODEL), f32)
    positions = jnp.broadcast_to(jnp.arange(SEQ, dtype=jnp.int32)[None, :], (BATCH, SEQ))
    return {
        "x": x,
        "positions": positions,
        "w_in": nrm(ks[1], (DEPTH, D_MODEL, IN_PROJ_WIDTH), D_MODEL ** -0.5),
        "b_glu": nrm(ks[2], (DEPTH, 2 * CONV_WIDTH), 0.02),
        "conv_w": nrm(ks[3], (DEPTH, CONV_KERNEL, CONV_WIDTH), CONV_KERNEL ** -0.5),
        "conv_b": nrm(ks[4], (DEPTH, CONV_WIDTH), 0.02),
        "conv_ln_g": 1.0 + nrm(ks[5], (DEPTH, CONV_WIDTH), 0.02),
        "conv_ln_b": nrm(ks[6], (DEPTH, CONV_WIDTH), 0.02),
        "w_out": nrm(ks[7], (DEPTH, MIX_WIDTH, D_MODEL), DEEPNORM_BETA * MIX_WIDTH ** -0.5),
        "ln1_g": 1.0 + nrm(ks[8], (DEPTH, D_MODEL), 0.02),
        "ln1_b": nrm(ks[9], (DEPTH, D_MODEL), 0.02),
        "w_ffn_in": nrm(ks[10], (DEPTH, D_MODEL, 2 * D_FF), D_MODEL ** -0.5),
        "ffn_conv_w": nrm(ks[11], (DEPTH, FFN_CONV_KERNEL, D_FF), FFN_CONV_KERNEL ** -0.5),
        "ffn_conv_b": nrm(ks[12], (DEPTH, D_FF), 0.02),
        "w_ffn_out": nrm(ks[13], (DEPTH, D_FF, D_MODEL), DEEPNORM_BETA * D_FF ** -0.5),
        "ln2_g": 1.0 + nrm(ks[14], (DEPTH, D_MODEL), 0.02),
        "ln2_b": nrm(ks[15], (DEPTH, D_MODEL), 0.02),
    }


def reference(x, positions, w_in, b_glu, conv_w, conv_b, conv_ln_g, conv_ln_b, w_out,
              ln1_g, ln1_b, w_ffn_in, ffn_conv_w, ffn_conv_b, w_ffn_out, ln2_g, ln2_b):
    for l in range(DEPTH):
        m = hybrid_mixer(x, positions, w_in[l], b_glu[l], conv_w[l], conv_b[l],
                         conv_ln_g[l], conv_ln_b[l], w_out[l])
        x = layer_norm(DEEPNORM_ALPHA * x + m, ln1_g[l], ln1_b[l])
        f = conv_gated_mlp(x, w_ffn_in[l], ffn_conv_w[l], ffn_conv_b[l], w_ffn_out[l])
        x = layer_norm(DEEPNORM_ALPHA * x + f, ln2_g[l], ln2_b[l])
    return x
```

```python
import contextlib
import numpy as np
import concourse.bass as bass
import concourse.mybir as mybir
from concourse.bass_utils import run_bass_kernel_spmd

F32 = mybir.dt.float32
BF16 = mybir.dt.bfloat16
I32 = mybir.dt.int32
ALU = mybir.AluOpType
AF = mybir.ActivationFunctionType

S = 4096
D = 1024
NT = S // 128
H = 12
AW = 768
CW = 256
IPW = 2816
DFF = 2816
NFC = DFF // 128
ALPHA = float(2.0 ** 0.25)
EPS = 1e-5
PI = float(np.pi)
INVF = [float(500000.0 ** (-i / 8.0)) for i in range(8)]
PPW = 160


class Op:
    __slots__ = ("eng", "fn", "deps", "semkey", "sigval", "is_dma", "needed", "ninc")

    def __init__(self, eng, fn, deps, semkey, is_dma):
        self.eng = eng
        self.fn = fn
        self.deps = deps
        self.semkey = semkey
        self.is_dma = is_dma
        self.needed = False
        self.sigval = None
        self.ninc = 1


def _flat(deps, out):
    if deps is None:
        return
    if isinstance(deps, Op):
        out.append(deps)
        return
    for d in deps:
        _flat(d, out)


class Prog:
    ENGS = ("pe", "act", "dve", "pool", "sp")

    def __init__(self, nc, stack):
        self.nc = nc
        self.stack = stack
        self.ops = {e: [] for e in self.ENGS}
        self.all = []
        self.last = {e: None for e in self.ENGS}
        self.pending = {e: None for e in self.ENGS}
        self.bar_pos = 0

    def op(self, eng, fn, deps=(), semkey=None, is_dma=False):
        flat = []
        _flat(deps, flat)
        if self.pending[eng] is not None:
            flat.extend(self.pending[eng])
            self.pending[eng] = None
        o = Op(eng, fn, flat, semkey, is_dma)
        self.ops[eng].append(o)
        self.all.append(o)
        if not is_dma:
            self.last[eng] = o
        return o

    def dma(self, fn, deps=(), semkey=None, eng="sp"):
        return self.op(eng, fn, deps, semkey=("dma", semkey), is_dma=True)

    def barrier(self):
        lasts = [self.last[e] for e in ("pe", "act", "dve", "pool") if self.last[e] is not None]
        lasts += [o for o in self.all[self.bar_pos:] if o.is_dma]
        self.bar_pos = len(self.all)
        for e in self.ENGS:
            self.pending[e] = list(lasts)

    def finalize(self, final_waits):
        nc = self.nc
        for o in self.all:
            for d in o.deps:
                if d.eng == "pe" and o.eng == "pe" and not d.is_dma:
                    continue
                d.needed = True
        sems = {}
        counts = {}
        for o in self.all:
            if o.is_dma:
                key = o.semkey
            elif o.needed:
                key = ("eng", o.eng)
            else:
                continue
            if key not in sems:
                sems[key] = self.stack.enter_context(nc.semaphore("s%d" % len(sems)))
                counts[key] = 0
            o.ninc = 16 if o.is_dma else 1
            counts[key] += o.ninc
            o.sigval = (sems[key], counts[key])
            o.needed = True
        self.final_waits = [o for o in self.all if o.is_dma]
        return len(sems)

    def emit(self, block):
        engmap = {"pe": block.tensor, "act": block.scalar, "dve": block.vector,
                  "pool": block.gpsimd, "sp": block.sync}
        final_waits = self.final_waits

        def make(engname):
            ops = self.ops[engname]

            def body(e):
                waited = {}
                for o in ops:
                    need = {}
                    for d in o.deps:
                        if d.eng == "pe" and o.eng == "pe" and not d.is_dma:
                            continue
                        sem, val = d.sigval
                        k = id(sem)
                        if k not in need or need[k][1] < val:
                            need[k] = (sem, val)
                    for k, (sem, val) in need.items():
                        if waited.get(k, 0) >= val:
                            continue
                        waited[k] = val
                        e.wait_ge(sem, val)
                    inst = o.fn(e)
                    if o.needed:
                        sem, val = o.sigval
                        inst.then_inc(sem, o.ninc)
                if engname == "sp":
                    need = {}
                    for d in final_waits:
                        sem, val = d.sigval
                        k = id(sem)
                        if k not in need or need[k][1] < val:
                            need[k] = (sem, val)
                    for k, (sem, val) in need.items():
                        if waited.get(k, 0) >= val:
                            continue
                        waited[k] = val
                        e.wait_ge(sem, val)
            return body

        for engname in self.ENGS:
            engmap[engname](make(engname))


class Ring:
    def __init__(self, bufs):
        self.bufs = list(bufs)
        self.free = [None] * len(self.bufs)
        self.held = [False] * len(self.bufs)
        self.i = 0

    def get(self):
        idx = self.i % len(self.bufs)
        self.i += 1
        assert not self.held[idx], "ring slot re-acquired before its release was recorded"
        self.held[idx] = True
        return idx, self.bufs[idx], self.free[idx]

    def release(self, idx, ops):
        self.free[idx] = ops
        self.held[idx] = False


def AP(t, off, dims):
    return bass.AP(t, off, [list(d) for d in dims])


def build(debug=False, stop_after="C"):
    nc = bass.Bass("TRN2", target_bir_lowering=False)
    KS = "ExternalOutput" if debug else "Internal"

    def dram(name, shape, dt, kind):
        return nc.dram_tensor(name, shape, dt, kind=kind)

    x_d = dram("x", [S, D], F32, "ExternalInput")
    pos_d = dram("pos", [128, NT], I32, "ExternalInput")
    win_d = dram("w_in", [D, IPW], F32, "ExternalInput")
    wout_d = dram("w_out", [D, D], F32, "ExternalInput")
    wfi_d = dram("w_fi", [D, 2 * DFF], F32, "ExternalInput")
    wfo_d = dram("w_fo", [DFF, D], F32, "ExternalInput")
    pp_d = dram("pp", [128, PPW], F32, "ExternalInput")
    bc_d = dram("bc", [1, 4 * D], F32, "ExternalInput")
    out_d = dram("out", [S, D], F32, "ExternalOutput")
    QTd = dram("QTd", [6, 128, S], BF16, KS)
    KTd = dram("KTd", [6, 128, S], BF16, KS)
    Vd = dram("Vd", [S, 12 * 128], BF16, KS)
    mixTd = dram("mixTd", [8, 128, S], BF16, KS)
    x1d = dram("x1d", [S, D], F32, KS)
    wfis_d = dram("wfis", [NFC, 128, 8 * 256], BF16, "Internal")

    with contextlib.ExitStack() as st:
        def sb(name, shape, dt):
            return st.enter_context(nc.sbuf_tensor("s_" + name, shape, dt))

        P = Prog(nc, st)
        ps = st.enter_context(nc.psum_tensor("ps", [128, 4096], F32))
        psb = ps.bitcast(BF16)

        def bank(b, n=512, off=0):
            return ps[:, b * 512 + off: b * 512 + off + n]

        ident = sb("ident", [128, 128], BF16)
        onesf = sb("onesf", [128, 128], F32)
        pp = sb("pp", [128, PPW], F32)
        bcv = sb("bcv", [128, 4 * D], F32)
        cs = sb("cs", [128, NT, 32], F32)
        mbase = sb("mbase", [128, 256], BF16)
        mstd = sb("mstd", [128, 1024], BF16)
        m16 = sb("m16", [128, 384], BF16)
        wout_b = sb("wout_b", [128, 8, D], BF16)
        epsT = sb("epsT", [128, 1], F32)
        mhalf = sb("mhalf", [128, 1], F32)
        stU = contextlib.ExitStack()
        uTb = stU.enter_context(nc.sbuf_tensor("u_uTb", [128, 2, S + 32], BF16))
        dg = stU.enter_context(nc.sbuf_tensor("u_dg", [128, 2, 31, 128], BF16))

        final_waits = []
        stW = contextlib.ExitStack()
        win_b = stW.enter_context(nc.sbuf_tensor("u_win_b", [128, 8, IPW], BF16))
        ld_win = [P.dma(lambda e, dc=dc: e.dma_start(out=win_b[:, dc, :], in_=win_d.ap()[dc * 128:(dc + 1) * 128, :]),
                        semkey="win%d" % dc, eng="pool") for dc in range(8)]

        ld_pp = P.dma(lambda e: e.dma_start(out=pp[:], in_=pp_d.ap()), semkey="pp")
        ld_bc = P.dma(lambda e: e.dma_start(out=bcv[:], in_=bc_d.ap().partition_broadcast(128)), semkey="bc")
        c_id0 = P.op("pool", lambda e: e.memset(ident[:], 0.0))
        c_id = P.op("pool", lambda e: e.affine_select(out=ident[:], in_=ident[:], compare_op=ALU.not_equal, fill=1.0,
                                                      base=0, pattern=[[-1, 128]], channel_multiplier=1), [c_id0])
        c_ones = P.op("pool", lambda e: e.memset(onesf[:], 1.0 / 256.0))
        c_eps = P.op("pool", lambda e: e.memset(epsT[:], EPS))
        c_mhalf = P.op("pool", lambda e: e.memset(mhalf[:], -0.5))
        u_zero = P.op("pool", lambda e: e.memset(uTb[:], 0.0))
        k0 = P.op("pool", lambda e: e.memset(mbase[:], 1.0))
        k1 = P.op("pool", lambda e: e.affine_select(out=mbase[:], in_=mbase[:], compare_op=ALU.is_ge, fill=0.0,
                                                    base=0, pattern=[[1, 256]], channel_multiplier=-1), [k0])
        k2 = P.op("pool", lambda e: e.affine_select(out=mbase[:], in_=mbase[:], compare_op=ALU.is_ge, fill=0.0,
                                                    base=128, pattern=[[-1, 256]], channel_multiplier=1), [k1])
        std_pieces = [(192, 64), (64, 192), (0, 256), (0, 256), (0, 192), (0, 64)]
        mops = []
        col = 0
        for (j0, n) in std_pieces:
            mops.append(P.op("pool", lambda e, c=col, j=j0, n=n: e.tensor_copy(out=mstd[:, c:c + n], in_=mbase[:, j:j + n]), [k2]))
            col += n
        mops.append(P.op("pool", lambda e: e.tensor_copy(out=m16[:, 0:192], in_=mbase[:, 64:256]), [k2]))
        mops.append(P.op("pool", lambda e: e.tensor_copy(out=m16[:, 192:384], in_=mbase[:, 0:192]), [k2]))
        mask_ready = mops

        with contextlib.ExitStack() as st0:
            pos_i = st0.enter_context(nc.sbuf_tensor("r_pos_i", [128, NT], I32))
            pos_f = st0.enter_context(nc.sbuf_tensor("r_pos_f", [128, NT], F32))
            invf = st0.enter_context(nc.sbuf_tensor("r_invf", [128, 8], F32))
            th = st0.enter_context(nc.sbuf_tensor("r_th", [128, 2, NT, 8], F32))
            kf = st0.enter_context(nc.sbuf_tensor("r_kf", [128, 2, NT, 8], F32))
            ki = st0.enter_context(nc.sbuf_tensor("r_ki", [128, 2, NT, 8], I32))
            sn = st0.enter_context(nc.sbuf_tensor("r_sn", [128, 2, NT, 8], F32))
            ld_pos = P.dma(lambda e: e.dma_start(out=pos_i[:], in_=pos_d.ap()), semkey="pos")
            r0 = P.op("dve", lambda e: e.tensor_copy(out=pos_f[:], in_=pos_i[:]), [ld_pos])
            ri = [P.op("dve", lambda e, i=i: e.memset(invf[:, i:i + 1], INVF[i])) for i in range(8)]
            rt = []
            for t in range(NT):
                rt.append(P.op("dve", lambda e, t=t: e.tensor_scalar_mul(out=th[:, 0, t, :], in0=invf[:], scalar1=pos_f[:, t:t + 1]),
                               [r0, ri]))
            r1 = P.op("dve", lambda e: e.tensor_scalar_add(out=th[:, 1], in0=th[:, 0], scalar1=PI / 2), [rt])
            r2 = P.op("dve", lambda e: e.tensor_scalar_mul(out=kf[:], in0=th[:], scalar1=float(1.0 / (2 * np.pi))), [r1])
            r3 = P.op("dve", lambda e: e.tensor_copy(out=ki[:], in_=kf[:]), [r2])
            r4 = P.op("dve", lambda e: e.tensor_copy(out=kf[:], in_=ki[:]), [r3])
            r5 = P.op("dve", lambda e: e.scalar_tensor_tensor(out=th[:], in0=kf[:], scalar=-6.28125, in1=th[:],
                                                              op0=ALU.mult, op1=ALU.add), [r4])
            r6 = P.op("dve", lambda e: e.scalar_tensor_tensor(out=th[:], in0=kf[:], scalar=-0.0019353071795864769, in1=th[:],
                                                              op0=ALU.mult, op1=ALU.add), [r5])
            r7 = P.op("dve", lambda e: e.tensor_single_scalar(out=kf[:], in_=th[:], scalar=PI, op=ALU.is_gt), [r6])
            r8 = P.op("dve", lambda e: e.scalar_tensor_tensor(out=th[:], in0=kf[:], scalar=-2 * PI, in1=th[:],
                                                              op0=ALU.mult, op1=ALU.add), [r7])
            r9 = P.op("dve", lambda e: e.tensor_single_scalar(out=kf[:], in_=th[:], scalar=-PI, op=ALU.is_lt), [r8])
            r10 = P.op("dve", lambda e: e.scalar_tensor_tensor(out=th[:], in0=kf[:], scalar=2 * PI, in1=th[:],
                                                               op0=ALU.mult, op1=ALU.add), [r9])
            r11 = P.op("dve", lambda e: e.tensor_scalar(out=th[:], in0=th[:], scalar1=PI, scalar2=-PI,
                                                        op0=ALU.min, op1=ALU.max), [r10])
            r12 = P.op("act", lambda e: e.activation(out=sn[:], in_=th[:], func=AF.Sin), [r11])
            t0 = P.op("dve", lambda e: e.tensor_copy(out=cs[:, :, 0:8], in_=sn[:, 1]), [r12])
            t1 = P.op("dve", lambda e: e.tensor_copy(out=cs[:, :, 8:16], in_=sn[:, 1]), [r12])
            t2 = P.op("dve", lambda e: e.tensor_scalar_mul(out=cs[:, :, 16:24], in0=sn[:, 0], scalar1=-1.0), [r12])
            t3 = P.op("dve", lambda e: e.tensor_copy(out=cs[:, :, 24:32], in_=sn[:, 0]), [r12])
            cs_ready = [t0, t1, t2, t3]

            stA = st0.enter_context(contextlib.ExitStack())

            def sba(name, shape, dt):
                return stA.enter_context(nc.sbuf_tensor("a_" + name, shape, dt))

            xf_r = Ring([sba("xf%d" % i, [128, D], F32) for i in range(4)])
            xb_r = Ring([sba("xb%d" % i, [128, D], BF16) for i in range(2)])
            xT_r = Ring([sba("xT%d" % i, [128, 8, 512], BF16) for i in range(2)])
            qk_r = Ring([sba("qk%d" % i, [128, 1536], BF16) for i in range(2)])
            vt_r = Ring([sba("vt%d" % i, [128, 12, 128], BF16) for i in range(2)])
            qs_r = Ring([sba("qs%d" % i, [128, 12, 512], BF16) for i in range(2)])
            ra_r = Ring([sba("ra%d" % i, [128, 8, 16], F32) for i in range(2)])
            rb_r = Ring([sba("rb%d" % i, [128, 8, 16], F32) for i in range(2)])
            sg_r = Ring([sba("sg%d" % i, [128, 512], F32) for i in range(2)])
            u_written = []
            vones = []
            for i, vt in enumerate(vt_r.bufs):
                vones.append(P.op("pool", lambda e, vt=vt: e.memset(vt[:], 1.0)))
                vt_r.free[i] = [vones[-1]]

            pj_r = Ring([0, 1, 2])
            dg_ops = [P.op("pool", lambda e, cc=cc, j=j: e.tensor_scalar_mul(out=dg[:, cc, j, :], in0=ident[:], scalar1=pp[:, cc * 31 + j: cc * 31 + j + 1]),
                           [c_id, ld_pp]) for cc in range(2) for j in range(31)]
            xtp_r = Ring([3])
            qkt_r = Ring([4, 5])
            cv_r = Ring([6, 7])
            qtd_st = []
            vd_st = []

            def load_x(tt):
                i, buf, fr = xf_r.get()
                o = P.dma(lambda e, b=buf, tt=tt: e.dma_start(out=b[:], in_=x_d.ap()[tt * 128:(tt + 1) * 128, :]),
                          fr, semkey="xf%d" % i)
                return i, buf, o

            pend_x = [load_x(tt) for tt in range(4)]
            for sg in range(8):
                xi, xT, xT_free = xT_r.get()
                qsi, qs, qs_free = qs_r.get()
                xT_done = []
                for k in range(4):
                    tt = sg * 4 + k
                    fi, xf, ldx = pend_x.pop(0)
                    bi, xb, xb_free = xb_r.get()
                    cst = P.op("act", lambda e, a=xb, b=xf: e.activation(out=a[:], in_=b[:], func=AF.Copy), [ldx, xb_free])
                    xf_r.release(fi, [cst])
                    if tt + 4 < NT:
                        pend_x.append(load_x(tt + 4))
                    pi_, pbank, pfree = xtp_r.get()
                    trs = []
                    for dc in range(8):
                        trs.append(P.op("pe", lambda e, dc=dc, xb=xb, pb=pbank: e.transpose(
                            out=psb[:, pb * 1024 + dc * 128: pb * 1024 + (dc + 1) * 128], in_=xb[:, dc * 128:(dc + 1) * 128],
                            identity=ident[:]), [cst, c_id, pfree]))
                    xb_r.release(bi, [trs[-1]])
                    ev = P.op("dve", lambda e, xT=xT, pb=pbank, k=k: e.tensor_copy(
                        out=xT[:, :, k * 128:(k + 1) * 128],
                        in_=psb[:, pb * 1024: pb * 1024 + 1024].rearrange("p (c t) -> p c t", t=128)), [trs[-1], xT_free])
                    xtp_r.release(pi_, [ev])
                    xT_done.append(ev)
                xT_readers = []
                for k in range(4):
                    tt = sg * 4 + k
                    qi, qk, qk_free = qk_r.get()
                    vi, vt, vt_free = vt_r.get()
                    qk_w = []
                    vt_w = []
                    for cg in range(5):
                        ncol = 512 if cg < 4 else 256
                        bi_, b, bfree = pj_r.get()
                        mm = None
                        for dc in range(8):
                            mm = P.op("pe", lambda e, b=b, dc=dc, k=k, cg=cg, ncol=ncol, xT=xT: e.matmul(
                                bank(b, ncol), lhsT=xT[:, dc, k * 128:(k + 1) * 128], rhs=win_b[:, dc, cg * 512: cg * 512 + ncol],
                                start=(dc == 0), stop=(dc == 7)), [xT_done[k], ld_win[dc], bfree])
                        xT_readers.append(mm)
                        if cg < 3:
                            cp = P.op("act", lambda e, b=b, qk=qk, cg=cg: e.activation(
                                out=qk[:, cg * 512:(cg + 1) * 512], in_=bank(b), func=AF.Copy), [mm, qk_free])
                            ai, ra, ra_free = ra_r.get()
                            bi2, rb, rb_free = rb_r.get()
                            pv3 = lambda b, lo, n: AP(ps, b * 512 + lo, [[4096, 128], [64, 8], [1, n]])
                            csb = lambda off, n, tt=tt: AP(cs, tt * 32 + off, [[NT * 32, 128], [0, 8], [1, n]])
                            o1 = P.op("dve", lambda e, b=b, ra=ra, pv3=pv3, csb=csb: e.tensor_tensor(
                                out=ra[:], in0=pv3(b, 0, 16), in1=csb(0, 16), op=ALU.mult), [mm, cs_ready, ra_free, cp])
                            o2 = P.op("dve", lambda e, b=b, rb=rb, pv3=pv3, csb=csb: e.tensor_tensor(
                                out=rb[:, :, 0:8], in0=pv3(b, 8, 8), in1=csb(16, 8), op=ALU.mult), [mm, cs_ready, rb_free, cp])
                            o3 = P.op("dve", lambda e, b=b, rb=rb, pv3=pv3, csb=csb: e.tensor_tensor(
                                out=rb[:, :, 8:16], in0=pv3(b, 0, 8), in1=csb(24, 8), op=ALU.mult), [mm, cs_ready, rb_free, cp])
                            qv = AP(qk, cg * 512, [[1536, 128], [64, 8], [1, 16]])
                            o4 = P.op("dve", lambda e, ra=ra, rb=rb, qv=qv: e.tensor_tensor(
                                out=qv, in0=ra[:], in1=rb[:], op=ALU.add), [o1, o2, o3, cp])
                            ra_r.release(ai, [o4])
                            rb_r.release(bi2, [o4])
                            pj_r.release(bi_, [cp, o1, o2, o3])
                            qk_w.append(o4)
                        else:
                            nh = 8 if cg == 3 else 4
                            h0 = 0 if cg == 3 else 8
                            src_e = AP(ps, b * 512, [[4096, 128], [128, nh // 2], [1, 64]])
                            src_o = AP(ps, b * 512 + 64, [[4096, 128], [128, nh // 2], [1, 64]])
                            dst_e = AP(vt, h0 * 128, [[1536, 128], [256, nh // 2], [1, 64]])
                            dst_o = AP(vt, h0 * 128 + 128 + 64, [[1536, 128], [256, nh // 2], [1, 64]])
                            ce = P.op("act", lambda e, s=src_e, d=dst_e: e.activation(out=d, in_=s, func=AF.Copy), [mm, vt_free])
                            co = P.op("dve", lambda e, s=src_o, d=dst_o: e.tensor_copy(out=d, in_=s), [mm, vt_free, ce])
                            pj_r.release(bi_, [ce, co])
                            vt_w += [ce, co]
                    vst = P.dma(lambda e, vt=vt, tt=tt: e.dma_start(
                        out=Vd.ap()[tt * 128:(tt + 1) * 128, :], in_=vt[:].rearrange("p h c -> p (h c)")), vt_w, semkey="vst%d" % vi)
                    vt_r.release(vi, [vst])
                    vd_st.append(vst)
                    qk_rd = []
                    for half in range(2):
                        ti, tb, tfree = qkt_r.get()
                        trs = []
                        for c6 in range(6):
                            c = half * 6 + c6
                            trs.append(P.op("pe", lambda e, tb=tb, c6=c6, c=c, qk=qk: e.transpose(
                                out=psb[:, tb * 1024 + c6 * 128: tb * 1024 + (c6 + 1) * 128], in_=qk[:, c * 128:(c + 1) * 128],
                                identity=ident[:]), [qk_w, tfree]))
                        eng = "act" if half == 0 else "dve"
                        if eng == "act":
                            ev = P.op("act", lambda e, tb=tb, half=half, qs=qs, k=k: e.activation(
                                out=qs[:, half * 6:(half + 1) * 6, k * 128:(k + 1) * 128],
                                in_=psb[:, tb * 1024: tb * 1024 + 768].rearrange("p (c t) -> p c t", t=128), func=AF.Copy),
                                [trs[-1], qs_free])
                        else:
                            ev = P.op("dve", lambda e, tb=tb, half=half, qs=qs, k=k: e.tensor_copy(
                                out=qs[:, half * 6:(half + 1) * 6, k * 128:(k + 1) * 128],
                                in_=psb[:, tb * 1024: tb * 1024 + 768].rearrange("p (c t) -> p c t", t=128)),
                                [trs[-1], qs_free])
                        qkt_r.release(ti, [ev])
                        qk_rd.append(trs[-1])
                        qtd_st.append(ev)
                    qk_r.release(qi, qk_rd)
                qs_evs = qtd_st[-8:]
                us_w = []
                for cc in range(2):
                    gi_, gb, gfree = cv_r.get()
                    mg = None
                    for dc in range(8):
                        mg = P.op("pe", lambda e, gb=gb, dc=dc, cc=cc, xT=xT: e.matmul(
                            bank(gb), lhsT=win_b[:, dc, 2560 + cc * 128: 2560 + (cc + 1) * 128], rhs=xT[:, dc, :],
                            start=(dc == 0), stop=(dc == 7)), [xT_done, ld_win[dc], gfree])
                    si, sgt, sg_free = sg_r.get()
                    sgo = P.op("act", lambda e, gb=gb, sgt=sgt, cc=cc: e.activation(
                        out=sgt[:], in_=bank(gb), func=AF.Sigmoid, bias=pp[:, 70 + cc:71 + cc], scale=1.0), [mg, sg_free, ld_pp])
                    cv_r.release(gi_, [sgo])
                    vi_, vb, vfree = cv_r.get()
                    mv = None
                    for dc in range(8):
                        mv = P.op("pe", lambda e, vb=vb, dc=dc, cc=cc, xT=xT: e.matmul(
                            bank(vb), lhsT=win_b[:, dc, 2304 + cc * 128: 2304 + (cc + 1) * 128], rhs=xT[:, dc, :],
                            start=(dc == 0), stop=(dc == 7)), [xT_done, ld_win[dc], vfree])
                    xT_readers += [mg, mv]
                    uo = P.op("dve", lambda e, vb=vb, sgt=sgt, cc=cc, sg=sg: e.scalar_tensor_tensor(
                        out=uTb[:, cc, 16 + sg * 512: 16 + (sg + 1) * 512], in0=bank(vb), scalar=pp[:, 68 + cc:69 + cc], in1=sgt[:],
                        op0=ALU.add, op1=ALU.mult), [mv, sgo, u_zero, ld_pp])
                    cv_r.release(vi_, [uo])
                    sg_r.release(si, [uo])
                    us_w.append(uo)
                xT_r.release(xi, xT_readers)
                st_ops = []
                st_ops.append(P.dma(lambda e, qs=qs, sg=sg: e.dma_start(
                    out=QTd.ap()[:, :, sg * 512:(sg + 1) * 512].rearrange("c p t -> p c t"), in_=qs[:, 0:6, :]),
                    qs_evs, semkey="qst%d" % qsi))
                st_ops.append(P.dma(lambda e, qs=qs, sg=sg: e.dma_start(
                    out=KTd.ap()[:, :, sg * 512:(sg + 1) * 512].rearrange("c p t -> p c t"), in_=qs[:, 6:12, :]),
                    qs_evs, semkey="qst%d" % qsi))
                qs_r.release(qsi, list(st_ops))
                final_waits += st_ops
                u_written += us_w
            final_waits += vd_st
            qk_stores = [o for o in final_waits if o.semkey[1].startswith("qst")]
            stA.close()
            P.barrier()
        stW.close()
        if stop_after == "A":
            return _finish(nc, P, final_waits)

        with contextlib.ExitStack() as stC:
            def sbc(name, shape, dt):
                return stC.enter_context(nc.sbuf_tensor("c_" + name, shape, dt))

            qz = [sbc("qz%d" % i, [128, S], BF16) for i in range(2)]
            qz_zero = [P.op("pool", lambda e: e.memset(qz[0][64:128, :], 0.0)), P.op("pool", lambda e: e.memset(qz[1][0:64, :], 0.0))]
            q_free = [None]
            kt_r = Ring([sbc("kt%d" % i, [128, S], BF16) for i in range(2)])
            v_r = Ring([sbc("v%d" % i, [128, 32, 256], BF16) for i in range(3)])
            acc = [sbc("acc%d" % i, [128, S], F32) for i in range(2)]
            acc_last = [None, None]
            pt_r = Ring([sbc("pt%d" % i, [128, 1024], BF16) for i in range(3)])
            ms_r = Ring([sbc("ms%d" % i, [128, S], BF16) for i in range(1)])
            dsh_r = Ring([sbc("dsh%d" % i, [128, 1024], F32) for i in range(2)])
            s_r = Ring([0, 2, 4])
            o_r = Ring([6, 7])

            MASK_MOD = 3
            LOOK = 2
            chunks = []
            for hp in range(6):
                for d in (1, 4, 16):
                    L = S // d
                    nTl = L // 128
                    for h2 in range(2):
                        for r in range(d):
                            for M0 in range(0, L, 512):
                                M1 = min(L, M0 + 512)
                                pieces = []
                                c0 = 64 if (d != 16 and M0 == 0) else 0
                                col = c0
                                for t in range(max(0, M0 // 128 - 1), min(nTl - 1, M1 // 128) + 1):
                                    qlo = max(M0, 128 * t - 64)
                                    qhi = min(M1, 128 * t + 192)
                                    if qhi <= qlo:
                                        continue
                                    pieces.append((t, qlo, qhi, col, qlo - (128 * t - 64)))
                                    col += qhi - qlo
                                chunks.append(dict(hp=hp, d=d, h2=h2, r=r, M0=M0, M1=M1, pieces=pieces, ncol=col - c0, c0=c0))

            hp_state = {}

            def load_hp(hp):
                ki_, kt, kfree = kt_r.get()
                lk = P.dma(lambda e, kt=kt, hp=hp: e.dma_start(out=kt[:], in_=KTd.ap()[hp]), [kfree, qk_stores], semkey="lk%d" % ki_)
                hp_state[hp] = dict(lq=None, ki=ki_, kt=kt, lk=lk, readers=[], v={})

            def load_q(hp):
                ops = [P.dma(lambda e, hp=hp: e.dma_start(out=qz[0][0:64, :], in_=QTd.ap()[hp, 0:64, :]), [q_free[0], qk_stores], semkey="lqA"),
                       P.dma(lambda e, hp=hp: e.dma_start(out=qz[1][64:128, :], in_=QTd.ap()[hp, 64:128, :]), [q_free[0], qk_stores], semkey="lqB")]
                hp_state[hp]["lq"] = ops

            def load_v(hp, d):
                vi, vb, vfree = v_r.get()
                ops = []
                vsrc = Vd.ap()[:, hp * 256:(hp + 1) * 256]
                if d == 1:
                    ops.append(P.dma(lambda e, vb=vb, vsrc=vsrc: e.dma_start(
                        out=vb[:], in_=vsrc.rearrange("(t p) c -> p t c", p=128)), [vfree, vd_st], semkey="lv%d" % vi))
                else:
                    nt = S // d // 128
                    v4 = vsrc.rearrange("(t p r) c -> r p t c", p=128, r=d)
                    for r in range(d):
                        ops.append(P.dma(lambda e, vb=vb, r=r, nt=nt, v4=v4: e.dma_start(
                            out=vb[:, r * nt:(r + 1) * nt, :], in_=v4[r]), [vfree, vd_st], semkey="lv%d" % vi))
                hp_state[hp]["v"][d] = dict(vi=vi, vb=vb, ld=ops, readers=[])

            ld_wout = [P.dma(lambda e, dc=dc: e.dma_start(out=wout_b[:, dc, :], in_=wout_d.ap()[dc * 128:(dc + 1) * 128, :]),
                             semkey="wout%d" % dc, eng="pool") for dc in range(8)]
            wfis_ops = []
            wfis_todo = []
            for c in range(NFC):
                for half in range(2):
                    col0 = half * DFF + c * 128
                    src = wfi_d.ap()[:, col0:col0 + 128].rearrange("(dc p) f -> p dc f", p=128)
                    dst = wfis_d.ap()[c].rearrange("p (dc f) -> p dc f", f=256)[:, :, half * 128:(half + 1) * 128]
                    wfis_todo.append((src, dst, c))

            def issue_wfis():
                if wfis_todo:
                    src, dst, c = wfis_todo.pop(0)
                    wfis_ops.append(P.dma(lambda e, s=src, d=dst: e.dma_start(out=d, in_=s), semkey="wfis%d" % (c % 4), eng="pool"))

            load_hp(0)
            load_v(0, 1)
            load_v(0, 4)
            load_v(0, 16)
            acc1b = acc[1].bitcast(BF16)
            cva_r = Ring([acc[0][:, i * 512:(i + 1) * 512] for i in range(4)])
            sq_r = Ring([acc[0][:, 2048 + i * 512: 2048 + (i + 1) * 512] for i in range(4)])
            m2 = acc[1][:, 0:512]
            rstd = acc[1][:, 512:1024]
            cn_r = Ring([acc[1][:, 1024 + i * 512: 1024 + (i + 1) * 512] for i in range(2)])
            co_r = Ring([acc1b[:, 4096 + i * 512: 4096 + (i + 1) * 512] for i in range(4)])
            cp_r = Ring([2, 3, 4, 5])
            prev_stat = None
            for tg in range(8):
                cvs = []
                sqs = []
                for cc in range(2):
                    bi_, cb_, cfree = cp_r.get()
                    mm = None
                    for j in range(31):
                        mm = P.op("pe", lambda e, cb_=cb_, cc=cc, j=j, tg=tg: e.matmul(
                            bank(cb_), lhsT=dg[:, cc, j, :], rhs=uTb[:, cc, tg * 512 + 1 + j: tg * 512 + 513 + j],
                            start=(j == 0), stop=(j == 30)), [dg_ops, u_written, cfree])
                    ai, cva, afree = cva_r.get()
                    cv = P.op("act", lambda e, cva=cva, cb_=cb_, cc=cc: e.activation(
                        out=cva[:], in_=bank(cb_), func=AF.Identity, bias=pp[:, 62 + cc:63 + cc], scale=1.0), [mm, afree, ld_pp])
                    cp_r.release(bi_, [cv])
                    si, sq, sfree = sq_r.get()
                    sqo = P.op("act", lambda e, sq=sq, cva=cva: e.activation(out=sq[:], in_=cva[:], func=AF.Square), [cv, sfree])
                    cvs.append((ai, cva, cv))
                    sqs.append((si, sq, sqo))
                mm_mean = None
                for cc in range(2):
                    mm_mean = P.op("pe", lambda e, cc=cc, t=cvs[cc][1]: e.matmul(bank(0), lhsT=onesf[:], rhs=t[:], start=(cc == 0), stop=(cc == 1)),
                                   [cvs[cc][2], c_ones, prev_stat])
                mm_msq = None
                for cc in range(2):
                    mm_msq = P.op("pe", lambda e, cc=cc, t=sqs[cc][1]: e.matmul(bank(1), lhsT=onesf[:], rhs=t[:], start=(cc == 0), stop=(cc == 1)),
                                  [sqs[cc][2], c_ones, prev_stat])
                for cc in range(2):
                    sq_r.release(sqs[cc][0], [mm_msq])
                s1 = P.op("act", lambda e: e.activation(out=m2[:], in_=bank(0), func=AF.Square), [mm_mean, prev_stat])
                s2 = P.op("dve", lambda e: e.tensor_tensor(out=rstd[:], in0=bank(1), in1=m2[:], op=ALU.subtract), [mm_msq, s1, prev_stat])
                s3 = P.op("act", lambda e: e.activation(out=rstd[:], in_=rstd[:], func=AF.Sqrt, bias=epsT[:], scale=1.0), [s2, c_eps])
                s4 = P.op("dve", lambda e: e.reciprocal(out=rstd[:], in_=rstd[:]), [s3])
                lastn = []
                for cc in range(2):
                    ni, cn, nfree = cn_r.get()
                    n1 = P.op("dve", lambda e, cn=cn, t=cvs[cc][1]: e.tensor_tensor(out=cn[:], in0=t[:], in1=bank(0), op=ALU.subtract),
                              [mm_mean, nfree, cvs[cc][2], s1])
                    cva_r.release(cvs[cc][0], [n1, mm_mean])
                    n2 = P.op("dve", lambda e, cn=cn: e.tensor_tensor(out=cn[:], in0=cn[:], in1=rstd[:], op=ALU.mult), [n1, s4])
                    oi, co, ofree = co_r.get()
                    n3 = P.op("act", lambda e, cn=cn, co=co, cc=cc: e.activation(
                        out=co[:], in_=cn[:], func=AF.Silu, bias=pp[:, 66 + cc:67 + cc], scale=pp[:, 64 + cc:65 + cc]), [n2, ofree])
                    cn_r.release(ni, [n3])
                    cst = P.dma(lambda e, co=co, cc=cc, tg=tg: e.dma_start(
                        out=mixTd.ap()[6 + cc, :, tg * 512:(tg + 1) * 512], in_=co[:]), [n3], semkey="cost%d" % oi)
                    co_r.release(oi, [cst])
                    final_waits.append(cst)
                    lastn += [n1, n2]
                prev_stat = lastn + [s4]
            conv_stores = [o for o in final_waits if o.semkey[1].startswith("cost")]
            P.barrier()
            if stop_after == "A2":
                return _finish(nc, P, final_waits)
            mix_st = {}
            n_ch = len(chunks)
            stage = [None] * n_ch

            def emit_S(ci):
                ch = chunks[ci]
                hs = hp_state[ch["hp"]]
                si, sb_, sfree = s_r.get()
                d, r, h2 = ch["d"], ch["r"], ch["h2"]
                if hs["lq"] is None:
                    if ch["hp"] > 0:
                        q_free[0] = list(hp_state[ch["hp"] - 1]["readers"])
                    load_q(ch["hp"])
                mm = None
                for (t, qlo, qhi, col, j0) in ch["pieces"]:
                    n = qhi - qlo
                    kv = AP(hs["kt"], r + d * 128 * t, [[S, 128], [d, 128]])
                    qv = AP(qz[h2], r + d * qlo, [[S, 128], [d, n]])
                    mm = P.op("pe", lambda e, sb_=sb_, col=col, n=n, kv=kv, qv=qv: e.matmul(
                        ps[:, sb_ * 512 + col: sb_ * 512 + col + n], lhsT=kv, rhs=qv, start=True, stop=True),
                        [hs["lq"], hs["lk"], sfree, qz_zero])
                hs["readers"].append(mm)
                stage[ci] = dict(si=si, sb=sb_, smm=mm)

            def emit_em(ci):
                ch = chunks[ci]
                stg = stage[ci]
                d = ch["d"]
                ncol = ch["ncol"]
                sb_ = stg["sb"]
                pi_, pt, pfree = pt_r.get()
                c0 = ch["c0"]
                ex = P.op("act", lambda e, pt=pt, sb_=sb_, ncol=ncol, c0=c0: e.activation(
                    out=pt[:, c0:c0 + ncol], in_=ps[:, sb_ * 512 + c0: sb_ * 512 + c0 + ncol], func=AF.Exp, scale=0.125), [stg["smm"], pfree])
                s_r.release(stg["si"], [ex])
                if d == 16:
                    mk = m16[:, 0:ncol]
                else:
                    first = ch["pieces"][0]
                    assert first[4] == (64 if c0 == 64 else 192), "mask layout"
                    mk = mstd[:, c0: c0 + ncol]
                mo = P.op(("pool" if (ci % 2) == 0 else "dve"), lambda e, pt=pt, mk=mk, ncol=ncol, c0=c0: e.tensor_tensor(
                    out=pt[:, c0:c0 + ncol], in0=pt[:, c0:c0 + ncol], in1=mk, op=ALU.mult), [ex, mask_ready])
                stg["pi"] = pi_
                stg["pt"] = pt
                stg["mo"] = mo

            def emit_pv(ci):
                ch = chunks[ci]
                stg = stage[ci]
                hs = hp_state[ch["hp"]]
                d, r, h2 = ch["d"], ch["r"], ch["h2"]
                vs = hs["v"][d]
                pt = stg["pt"]
                oi, ob, ofree = o_r.get()
                nT = (S // d) // 128
                pvm = None
                for k, (t, qlo, qhi, col, j0) in enumerate(ch["pieces"]):
                    n = qhi - qlo
                    vv = vs["vb"][:, r * nT + t, h2 * 128:(h2 + 1) * 128]
                    pvm = P.op("pe", lambda e, ob=ob, vv=vv, pt=pt, col=col, n=n, o0=qlo - ch["M0"], k=k: e.matmul(
                        ps[:, ob * 512 + o0: ob * 512 + o0 + n], lhsT=vv, rhs=pt[:, col:col + n],
                        start=(k == 0), stop=True, skip_group_check=True), [stg["mo"], vs["ld"], ofree])
                pt_r.release(stg["pi"], [pvm])
                vs["readers"].append(pvm)
                stg["oi"] = oi
                stg["ob"] = ob
                stg["pvm"] = pvm

            def emit_evac(ci):
                ch = chunks[ci]
                stg = stage[ci]
                d, r, h2 = ch["d"], ch["r"], ch["h2"]
                ob, pvm = stg["ob"], stg["pvm"]
                nq = ch["M1"] - ch["M0"]
                a = acc[h2]
                av = AP(a, r + d * ch["M0"], [[S, 128], [d, nq]])
                if d == 1 and (ci % 2) == 1:
                    ev = P.op("act", lambda e, av=av, ob=ob, nq=nq: e.activation(out=av, in_=ps[:, ob * 512: ob * 512 + nq], func=AF.Copy),
                              [pvm, acc_last[h2]])
                elif d == 1:
                    ev = P.op("dve", lambda e, av=av, ob=ob, nq=nq: e.tensor_copy(out=av, in_=ps[:, ob * 512: ob * 512 + nq]),
                              [pvm, acc_last[h2]])
                else:
                    ev = P.op("dve", lambda e, av=av, ob=ob, nq=nq: e.tensor_tensor(
                        out=av, in0=ps[:, ob * 512: ob * 512 + nq], in1=av, op=ALU.add), [pvm, acc_last[h2]])
                acc_last[h2] = ev
                o_r.release(stg["oi"], [ev])
                stage[ci] = None
                return ev

            def finish_head(hp, h2, last_ev):
                if hp not in mix_st:
                    mi, ms, mfree = ms_r.get()
                    mix_st[hp] = dict(mi=mi, ms=ms, mfree=mfree, w=[])
                m = mix_st[hp]
                num = slice(0, 64) if h2 == 0 else slice(64, 128)
                den = slice(64, 128) if h2 == 0 else slice(0, 64)
                a = acc[h2]
                last = last_ev
                for q in range(4):
                    di, dsh, dfree = dsh_r.get()
                    c0_ = P.op("act", lambda e, dsh=dsh, a=a, q=q, num=num, den=den: e.activation(
                        out=dsh[num, :], in_=a[den, q * 1024:(q + 1) * 1024], func=AF.Ln), [last_ev, dfree])
                    c1 = P.op("act", lambda e, dsh=dsh, num=num: e.activation(
                        out=dsh[num, :], in_=dsh[num, :], func=AF.Exp, scale=-1.0), [c0_])
                    c2 = P.op("dve", lambda e, dsh=dsh, a=a, q=q, num=num, ms=m["ms"]: e.tensor_tensor(
                        out=ms[num, q * 1024:(q + 1) * 1024], in0=a[num, q * 1024:(q + 1) * 1024], in1=dsh[num, :], op=ALU.mult),
                        [c1, m["mfree"]])
                    dsh_r.release(di, [c2])
                    m["w"].append(c2)
                    last = c2
                acc_last[h2] = last
                if h2 == 1:
                    stq = P.dma(lambda e, ms=m["ms"], hp=hp: e.dma_start(out=mixTd.ap()[hp], in_=ms[:]), m["w"], semkey="mst%d" % m["mi"])
                    ms_r.release(m["mi"], [stq])
                    final_waits.append(stq)

            def chunk_done(ci, ev):
                ch = chunks[ci]
                last_of_head = (ci + 1 == n_ch) or (chunks[ci + 1]["h2"] != ch["h2"]) or (chunks[ci + 1]["d"] != ch["d"])
                last_of_pat = (ci + 1 == n_ch) or (chunks[ci + 1]["d"] != ch["d"]) or (chunks[ci + 1]["hp"] != ch["hp"])
                if last_of_pat:
                    vs = hp_state[ch["hp"]]["v"][ch["d"]]
                    v_r.release(vs["vi"], list(vs["readers"]))
                    if ch["hp"] + 1 < 6:
                        if ch["d"] == 1:
                            load_hp(ch["hp"] + 1)
                        load_v(ch["hp"] + 1, ch["d"])
                if ch["d"] == 16 and last_of_head:
                    finish_head(ch["hp"], ch["h2"], ev)
                if ch["d"] == 16 and last_of_pat:
                    hs = hp_state[ch["hp"]]
                    kt_r.release(hs["ki"], list(hs["readers"]))

            for t in range(-2, n_ch + 1):
                if 0 <= t + 2 < n_ch:
                    emit_S(t + 2)
                if 0 <= t + 1 < n_ch:
                    emit_em(t + 1)
                if 0 <= t < n_ch:
                    emit_pv(t)
                if 0 <= t - 1 < n_ch:
                    ev = emit_evac(t - 1)
                    chunk_done(t - 1, ev)
                if t >= 0 and t % 6 == 0:
                    issue_wfis()
            while wfis_todo:
                issue_wfis()
            mix_stores = [o for o in final_waits if o.semkey[1].startswith("mst")]
            P.barrier()
        stU.close()
        if stop_after == "B":
            return _finish(nc, P, final_waits)

        with contextlib.ExitStack() as stD:
            def sbd(name, shape, dt):
                return stD.enter_context(nc.sbuf_tensor("d_" + name, shape, dt))

            wfo_b = sbd("wfo_b", [128, NFC, D], BF16)
            ld_wfo = []

            def issue_wfo(c):
                ld_wfo.append(P.dma(lambda e: e.dma_start(out=wfo_b[:, c, :], in_=wfo_d.ap()[c * 128:(c + 1) * 128, :]),
                                    semkey="wfo%d" % (c % 8), eng="pool"))
            mx_r = Ring([sbd("mx%d" % i, [128, 8, 128], BF16) for i in range(3)])
            STQ = "pool"
            xr_r = Ring([sbd("xr%d" % i, [128, D], F32) for i in range(1)])
            y_r = Ring([sbd("y%d" % i, [128, D], F32) for i in range(2)])
            x1f_r = Ring([sbd("x1f%d" % i, [128, D], F32) for i in range(2)])
            x1b_r = Ring([sbd("x1b%d" % i, [128, D], BF16) for i in range(2)])
            x1T_sl = [sbd("x1T%d" % i, [128, 8, 512], BF16) for i in range(3)]
            xh = [sbd("xh%d" % i, [128, 8, 2], BF16) for i in range(8)]
            aT = sbd("aT", [128, NFC, 512], BF16)
            gext_r = Ring([sbd("gext%d" % i, [128, 514], F32) for i in range(2)])
            tt_r = Ring([sbd("tt%d" % i, [128, 512], F32) for i in range(2)])
            ss_r = Ring([sbd("ss%d" % i, [128, 512], F32) for i in range(2)])
            us_r = Ring([sbd("us%d" % i, [128, 512], F32) for i in range(2)])
            wb_r = Ring([sbd("wb%d" % i, [128, 8, 256], BF16) for i in range(3)])
            x1r_r = Ring([sbd("x1r%d" % i, [128, D], F32) for i in range(2)])
            ot_r = Ring([sbd("ot%d" % i, [128, D], F32) for i in range(2)])
            st_r = Ring([sbd("stt%d" % i, [128, 8], F32) for i in range(4)])
            junk = sbd("junk", [128, D], BF16)
            junk_last = [None]
            xh_init = [P.op("pool", lambda e, t=t: e.memset(t[:], 0.0)) for t in xh]
            xh_w = [[xh_init[i]] for i in range(8)]
            mf_r = Ring([0, 1, 2])
            h_r = Ring([3, 4, 5, 6])
            tp_r = mf_r
            hl_r = Ring([7])
            aT_free = [None]
            out_stores = []
            x1_stores = {}

            def layer_norm_steps(banks, resid, resid_dep, gcol, bcol, extra_deps):
                stx = {}

                def step1():
                    yi, y, yfree = y_r.get()
                    si, stt, sfree = st_r.get()
                    a0a = P.op("dve", lambda e: e.scalar_tensor_tensor(
                        out=y[:, 0:512], in0=resid[:, 0:512], scalar=ALPHA, in1=bank(banks[0]),
                        op0=ALU.mult, op1=ALU.add), [resid_dep, yfree, extra_deps])
                    a0 = P.op("dve", lambda e: e.scalar_tensor_tensor(
                        out=y[:, 512:1024], in0=resid[:, 512:1024], scalar=ALPHA, in1=bank(banks[1]),
                        op0=ALU.mult, op1=ALU.add), [resid_dep, yfree, extra_deps, a0a])
                    a1 = P.op("act", lambda e: e.activation(out=junk[:], in_=y[:], func=AF.Copy, accum_out=stt[:, 0:1]), [a0, sfree, junk_last[0]])
                    a2 = P.op("act", lambda e: e.activation(out=junk[:], in_=y[:], func=AF.Square, accum_out=stt[:, 1:2]), [a0, a1])
                    junk_last[0] = a2
                    stx.update(yi=yi, y=y, si=si, stt=stt, a1=a1, a2=a2)
                    return (a0a, a0)

                def step2():
                    stt, a1, a2 = stx["stt"], stx["a1"], stx["a2"]
                    b0 = P.op("dve", lambda e: e.tensor_scalar_mul(out=stt[:, 2:3], in0=stt[:, 0:1], scalar1=1.0 / D), [a1])
                    b1 = P.op("dve", lambda e: e.tensor_tensor(out=stt[:, 3:4], in0=stt[:, 2:3], in1=stt[:, 2:3], op=ALU.mult), [b0])
                    b2 = P.op("dve", lambda e: e.scalar_tensor_tensor(out=stt[:, 4:5], in0=stt[:, 1:2], scalar=1.0 / D, in1=stt[:, 3:4],
                                                                      op0=ALU.mult, op1=ALU.subtract), [b1, a2])
                    b3a = P.op("dve", lambda e: e.tensor_scalar_add(out=stt[:, 5:6], in0=stt[:, 4:5], scalar1=EPS), [b2])
                    b3 = P.op("pool", lambda e: e.tensor_tensor(out=stt[:, 6:7], in0=stt[:, 5:6], in1=mhalf[:], op=ALU.pow), [b3a, c_mhalf])
                    stx["b3"] = b3

                def step3(out_t, out_free):
                    stt, y, a2, b3 = stx["stt"], stx["y"], stx["a2"], stx["b3"]
                    n0 = P.op("dve", lambda e: e.tensor_scalar(out=y[:], in0=y[:], scalar1=stt[:, 2:3], scalar2=stt[:, 6:7],
                                                               op0=ALU.subtract, op1=ALU.mult), [b3, a2])
                    n1 = P.op("dve", lambda e: e.tensor_tensor(out=y[:], in0=y[:], in1=bcv[:, gcol * D:(gcol + 1) * D], op=ALU.mult), [n0, ld_bc])
                    n2 = P.op("dve", lambda e: e.tensor_tensor(out=out_t[:], in0=y[:], in1=bcv[:, bcol * D:(bcol + 1) * D], op=ALU.add), [n1, out_free])
                    y_r.release(stx["yi"], [n2])
                    st_r.release(stx["si"], [n0])
                    return n2

                return step1, step2, step3

            def layer_norm(banks, resid, resid_dep, gcol, bcol, out_t, out_free, extra_deps):
                s1_, s2_, s3_ = layer_norm_steps(banks, resid, resid_dep, gcol, bcol, extra_deps)
                a0 = s1_()
                s2_()
                n2 = s3_(out_t, out_free)
                return a0, n2

            NSLOT = 3
            slot_free = [None] * NSLOT
            tile_ev = {}
            s1st = {}

            s1a = {}
            s1b = {}

            def s1_A1(j):
                mi, mx, mfree = mx_r.get()
                ldm = P.dma(lambda e: e.dma_start(out=mx[:], in_=mixTd.ap()[:, :, j * 128:(j + 1) * 128].rearrange("c p t -> p c t")),
                            [mfree, mix_stores, conv_stores], semkey="mx%d" % mi)
                ri_, xr, rfree = xr_r.get()
                ldx = P.dma(lambda e: e.dma_start(out=xr[:], in_=x_d.ap()[j * 128:(j + 1) * 128, :]), [rfree], semkey="xr%d" % ri_)
                bi_ = []
                b = []
                mm = None
                for half in range(2):
                    i_, b_, bfree = mf_r.get()
                    bi_.append(i_)
                    b.append(b_)
                    for c in range(8):
                        mm = P.op("pe", lambda e, half=half, c=c, b_=b_: e.matmul(
                            bank(b_), lhsT=mx[:, c, :], rhs=wout_b[:, c, half * 512:(half + 1) * 512],
                            start=(c == 0), stop=(c == 7)), [ldm, ld_wout, bfree])
                mx_r.release(mi, [mm])
                s1a[j] = (ri_, xr, ldx, bi_, b, mm)

            s1ln = {}

            def s1_A2a(j):
                ri_, xr, ldx, bi_, b, mm = s1a.pop(j)
                st1, st2, st3 = layer_norm_steps(b, xr, ldx, 0, 1, [mm])
                a0 = st1()
                mf_r.release(bi_[0], [a0[0]])
                mf_r.release(bi_[1], [a0[1]])
                xr_r.release(ri_, [a0[1]])
                s1ln[j] = (st2, st3)

            def s1_A2b(j):
                s1ln[j][0]()

            def s1_A2c(j):
                st3 = s1ln.pop(j)[1]
                fi, x1f, ffree = x1f_r.get()
                n2 = st3(x1f, ffree)
                x1s = P.dma(lambda e: e.dma_start(out=x1d.ap()[j * 128:(j + 1) * 128, :], in_=x1f[:]), [n2], semkey="x1s%d" % fi, eng=STQ)
                x1_stores[j] = x1s
                bi2, x1b, bbfree = x1b_r.get()
                cb = P.op("act", lambda e: e.activation(out=x1b[:], in_=x1f[:], func=AF.Copy), [n2, bbfree])
                x1f_r.release(fi, [x1s, cb])
                s1st[j] = (bi2, x1b, cb)

            def s1_A2(j):
                s1_A2a(j)
                s1_A2b(j)
                s1_A2c(j)

            def s1_B1(j):
                bi2, x1b, cb = s1st.pop(j)
                g, k = j // 4, j % 4
                x1T = x1T_sl[g % NSLOT]
                ti, tb, tfree = tp_r.get()
                trs = None
                for dc in range(8):
                    trs = P.op("pe", lambda e, dc=dc: e.transpose(
                        out=psb[:, tb * 1024 + dc * 128: tb * 1024 + (dc + 1) * 128], in_=x1b[:, dc * 128:(dc + 1) * 128],
                        identity=ident[:]), [cb, tfree])
                x1b_r.release(bi2, [trs])
                s1b[j] = (ti, tb, trs)

            def s1_B2(j):
                ti, tb, trs = s1b.pop(j)
                g, k = j // 4, j % 4
                x1T = x1T_sl[g % NSLOT]
                ev = P.op("dve", lambda e: e.tensor_copy(
                    out=x1T[:, :, k * 128:(k + 1) * 128],
                    in_=psb[:, tb * 1024: tb * 1024 + 1024].rearrange("p (c t) -> p c t", t=128)), [trs, slot_free[g % NSLOT]])
                tp_r.release(ti, [ev])
                tile_ev[j] = ev
                if k == 0 and g > 0:
                    xh_w[g - 1].append(P.op("pool", lambda e: e.tensor_copy(out=xh[g - 1][:, :, 1:2], in_=x1T[:, :, 0:1]), [ev, xh_w[g - 1]]))
                if k == 3 and g < 7:
                    xh_w[g + 1].append(P.op("pool", lambda e: e.tensor_copy(out=xh[g + 1][:, :, 0:1], in_=x1T[:, :, 511:512]), [ev, xh_w[g + 1]]))

            def stage2(g):
                x1T = x1T_sl[g % NSLOT]
                s1evs = [tile_ev[4 * g + k] for k in range(4)]
                hooks = {}
                for i, j in enumerate(range(4 * g + 5, 4 * g + 9)):
                    if j < NT:
                        for off, fn in ((0, s1_A1), (1, s1_A2a), (2, s1_A2b), (3, s1_A2c), (5, s1_B1), (6, s1_B2)):
                            hooks.setdefault(5 * i + off, []).append((fn, j))
                x1T_rd = []
                a_w = []
                for c in range(NFC):
                    wi, wb, wfree = wb_r.get()
                    ldw = P.dma(lambda e, wb=wb, c=c: e.dma_start(out=wb[:].rearrange("p a b -> p (a b)"), in_=wfis_d.ap()[c]),
                                [wfree, wfis_ops], semkey="wb%d" % wi)
                    hi, hb, hfree = h_r.get()
                    ui_, ub_, ufree_ = h_r.get()
                    li, lb, lfree = hl_r.get()
                    mg = None
                    for dc in range(8):
                        mg = P.op("pe", lambda e, dc=dc, wb=wb, hb=hb: e.matmul(
                            bank(hb), lhsT=wb[:, dc, 0:128], rhs=x1T[:, dc, :], start=(dc == 0), stop=(dc == 7)),
                            [ldw, s1evs, hfree])
                    mu = None
                    for dc in range(8):
                        mu = P.op("pe", lambda e, dc=dc, wb=wb, ub_=ub_: e.matmul(
                            bank(ub_), lhsT=wb[:, dc, 128:256], rhs=x1T[:, dc, :], start=(dc == 0), stop=(dc == 7)), [ufree_])
                    mh = None
                    for dc in range(8):
                        mh = P.op("pe", lambda e, dc=dc, wb=wb, lb=lb: e.matmul(
                            ps[:, lb * 512: lb * 512 + 2], lhsT=wb[:, dc, 0:128], rhs=xh[g][:, dc, :], start=(dc == 0), stop=(dc == 7)),
                            [xh_w[g], lfree])
                    wb_r.release(wi, [mh])
                    x1T_rd.append(mu)
                    ti, tt_, tfree = tt_r.get()
                    si, ss, sfree = ss_r.get()
                    e2 = P.op("act", lambda e, tt_=tt_, hb=hb, c=c: e.activation(
                        out=tt_[:], in_=bank(hb), func=AF.Identity, scale=pp[:, 72 + c * 3 + 1: 72 + c * 3 + 2],
                        bias=pp[:, 138 + c:139 + c]), [mg, tfree, ld_pp])
                    e3 = P.op("dve", lambda e, tt_=tt_, hb=hb, c=c: e.scalar_tensor_tensor(
                        out=tt_[:, 1:512], in0=bank(hb, 511, 0), scalar=pp[:, 72 + c * 3: 72 + c * 3 + 1], in1=tt_[:, 1:512],
                        op0=ALU.mult, op1=ALU.add), [e2])
                    e4 = P.op("dve", lambda e, tt_=tt_, hb=hb, c=c: e.scalar_tensor_tensor(
                        out=tt_[:, 0:511], in0=bank(hb, 511, 1), scalar=pp[:, 72 + c * 3 + 2: 72 + c * 3 + 3], in1=tt_[:, 0:511],
                        op0=ALU.mult, op1=ALU.add), [e3])
                    e3b = P.op("dve", lambda e, tt_=tt_, lb=lb, c=c: e.scalar_tensor_tensor(
                        out=tt_[:, 0:1], in0=ps[:, lb * 512: lb * 512 + 1], scalar=pp[:, 72 + c * 3: 72 + c * 3 + 1], in1=tt_[:, 0:1],
                        op0=ALU.mult, op1=ALU.add), [e4, mh])
                    e1 = P.op("dve", lambda e, tt_=tt_, lb=lb, c=c: e.scalar_tensor_tensor(
                        out=tt_[:, 511:512], in0=ps[:, lb * 512 + 1: lb * 512 + 2], scalar=pp[:, 72 + c * 3 + 2: 72 + c * 3 + 3], in1=tt_[:, 511:512],
                        op0=ALU.mult, op1=ALU.add), [e3b])
                    hl_r.release(li, [e1])
                    e0 = e4
                    e4 = e1
                    e5 = P.op("act", lambda e, ss=ss, tt_=tt_: e.activation(out=ss[:], in_=tt_[:], func=AF.Silu), [e4, sfree])
                    tt_r.release(ti, [e5])
                    vi_, us_, vfree_ = us_r.get()
                    eu = P.op("act", lambda e, us_=us_, ub_=ub_: e.activation(out=us_[:], in_=bank(ub_), func=AF.Copy), [mu, vfree_])
                    e6 = P.op("pool", lambda e, ss=ss, us_=us_, c=c: e.tensor_tensor(
                        out=aT[:, c, :], in0=us_[:], in1=ss[:], op=ALU.mult), [e5, eu, aT_free[0]])
                    ss_r.release(si, [e6])
                    us_r.release(vi_, [e6])
                    h_r.release(hi, [e0, e2])
                    h_r.release(ui_, [eu])
                    a_w.append(e6)
                    for fn, jj in hooks.get(c, []):
                        fn(jj)
                    if g == 0:
                        issue_wfo(c)
                slot_free[g % NSLOT] = x1T_rd
                last_mm = None
                pend = None

                def ln2(args):
                    tt, ri_, x1r, ldx, bi_, b, mm = args
                    oi, ot, ofree = ot_r.get()
                    a0, n2 = layer_norm(b, x1r, ldx, 2, 3, ot, ofree, [mm])
                    mf_r.release(bi_[0], [a0[0]])
                    mf_r.release(bi_[1], [a0[1]])
                    x1r_r.release(ri_, [a0[1]])
                    ost = P.dma(lambda e: e.dma_start(out=out_d.ap()[tt * 128:(tt + 1) * 128, :], in_=ot[:]), [n2], semkey="ost%d" % oi, eng=STQ)
                    ot_r.release(oi, [ost])
                    out_stores.append(ost)

                for k in range(4):
                    tt = g * 4 + k
                    ri_, x1r, rfree = x1r_r.get()
                    ldx = P.dma(lambda e, x1r=x1r, tt=tt: e.dma_start(out=x1r[:], in_=x1d.ap()[tt * 128:(tt + 1) * 128, :]),
                                [rfree, x1_stores[tt]], semkey="x1r%d" % ri_)
                    bi_ = []
                    b = []
                    mm = None
                    for half in range(2):
                        if half == 1 and pend is not None:
                            ln2(pend)
                            pend = None
                        i_, b_, bfree = mf_r.get()
                        bi_.append(i_)
                        b.append(b_)
                        for c in range(NFC):
                            mm = P.op("pe", lambda e, half=half, c=c, k=k, b_=b_: e.matmul(
                                bank(b_), lhsT=aT[:, c, k * 128:(k + 1) * 128], rhs=wfo_b[:, c, half * 512:(half + 1) * 512],
                                start=(c == 0), stop=(c == NFC - 1)), [a_w[c], ld_wfo, bfree])
                    last_mm = mm
                    pend = (tt, ri_, x1r, ldx, bi_, b, mm)
                ln2(pend)
                aT_free[0] = [last_mm]

            steps = (s1_A1, s1_A2a, s1_A2b, s1_A2c, s1_B1, s1_B2)
            for sl in range(0, 2 * 4 + 6):
                for j in range(5):
                    k = sl - 2 * j
                    if 0 <= k < 6:
                        steps[k](j)
            for g in range(8):
                stage2(g)
            final_waits += out_stores
        return _finish(nc, P, final_waits)


def _finish(nc, P, final_waits):
    nsem = P.finalize(final_waits)
    with nc.Block() as block:
        P.emit(block)
    return nc


_NC_CACHE = {}


def _pack_small(inp):
    pp = np.zeros((128, PPW), np.float32)
    cw = np.asarray(inp["conv_w"], np.float32)[0]
    for cc in range(2):
        pp[:, cc * 31:(cc + 1) * 31] = cw[:, cc * 128:(cc + 1) * 128].T
    for k, name in ((62, "conv_b"), (64, "conv_ln_g"), (66, "conv_ln_b")):
        v = np.asarray(inp[name], np.float32)[0]
        pp[:, k:k + 2] = v.reshape(2, 128).T
    bg = np.asarray(inp["b_glu"], np.float32)[0]
    pp[:, 68:70] = bg[:256].reshape(2, 128).T
    pp[:, 70:72] = bg[256:].reshape(2, 128).T
    fw = np.asarray(inp["ffn_conv_w"], np.float32)[0]
    pp[:, 72:138] = fw.T.reshape(NFC, 128, 3).transpose(1, 0, 2).reshape(128, NFC * 3)
    fb = np.asarray(inp["ffn_conv_b"], np.float32)[0]
    pp[:, 138:160] = fb.reshape(NFC, 128).T
    bc = np.concatenate([np.asarray(inp[n], np.float32)[0] for n in ("ln1_g", "ln1_b", "ln2_g", "ln2_b")])[None, :]
    return pp, np.ascontiguousarray(bc)


def make_in_maps(inp, n_cores=8):
    pp, bc = _pack_small(inp)
    x = np.asarray(inp["x"], np.float32)
    pos = np.asarray(inp["positions"], np.int32)
    shared = {
        "w_in": np.ascontiguousarray(np.asarray(inp["w_in"], np.float32)[0]),
        "w_out": np.ascontiguousarray(np.asarray(inp["w_out"], np.float32)[0]),
        "w_fi": np.ascontiguousarray(np.asarray(inp["w_ffn_in"], np.float32)[0]),
        "w_fo": np.ascontiguousarray(np.asarray(inp["w_ffn_out"], np.float32)[0]),
        "pp": pp, "bc": bc,
    }
    maps = []
    for b in range(n_cores):
        m = dict(shared)
        m["x"] = np.ascontiguousarray(x[b])
        m["pos"] = np.ascontiguousarray(pos[b].reshape(NT, 128).T)
        maps.append(m)
    return maps


def kernel(**inputs):
    if "nc" not in _NC_CACHE:
        _NC_CACHE["nc"] = build()
    nc = _NC_CACHE["nc"]
    maps = make_in_maps(inputs, 8)
    res = run_bass_kernel_spmd(nc, maps, core_ids=list(range(8)))
    return np.stack([np.asarray(r["out"], np.float32) for r in res.results], axis=0)
```

```python
import contextlib
import numpy as np
import concourse.bass as bass
import concourse.mybir as mybir
from concourse.bass_utils import run_bass_kernel_spmd

F32 = mybir.dt.float32
BF16 = mybir.dt.bfloat16
I32 = mybir.dt.int32
ALU = mybir.AluOpType
AF = mybir.ActivationFunctionType

S = 4096
D = 1024
NT = S // 128
H = 12
AW = 768
CW = 256
IPW = 2816
DFF = 2816
NFC = DFF // 128
ALPHA = float(2.0 ** 0.25)
EPS = 1e-5
PI = float(np.pi)
INVF = [float(500000.0 ** (-i / 8.0)) for i in range(8)]
PPW = 160


class Op:
    __slots__ = ("eng", "fn", "deps", "semkey", "sigval", "is_dma", "needed", "ninc")

    def __init__(self, eng, fn, deps, semkey, is_dma):
        self.eng = eng
        self.fn = fn
        self.deps = deps
        self.semkey = semkey
        self.is_dma = is_dma
        self.needed = False
        self.sigval = None
        self.ninc = 1


def _flat(deps, out):
    if deps is None:
        return
    if isinstance(deps, Op):
        out.append(deps)
        return
    for d in deps:
        _flat(d, out)


class Prog:
    ENGS = ("pe", "act", "dve", "pool", "sp")

    def __init__(self, nc, stack):
        self.nc = nc
        self.stack = stack
        self.ops = {e: [] for e in self.ENGS}
        self.all = []
        self.last = {e: None for e in self.ENGS}
        self.pending = {e: None for e in self.ENGS}
        self.bar_pos = 0

    def op(self, eng, fn, deps=(), semkey=None, is_dma=False):
        flat = []
        _flat(deps, flat)
        if self.pending[eng] is not None:
            flat.extend(self.pending[eng])
            self.pending[eng] = None
        o = Op(eng, fn, flat, semkey, is_dma)
        self.ops[eng].append(o)
        self.all.append(o)
        if not is_dma:
            self.last[eng] = o
        return o

    def dma(self, fn, deps=(), semkey=None, eng="sp"):
        return self.op(eng, fn, deps, semkey=("dma", semkey), is_dma=True)

    def barrier(self):
        lasts = [self.last[e] for e in ("pe", "act", "dve", "pool") if self.last[e] is not None]
        lasts += [o for o in self.all[self.bar_pos:] if o.is_dma]
        self.bar_pos = len(self.all)
        for e in self.ENGS:
            self.pending[e] = list(lasts)

    def finalize(self, final_waits):
        nc = self.nc
        for o in self.all:
            for d in o.deps:
                if d.eng == "pe" and o.eng == "pe" and not d.is_dma:
                    continue
                d.needed = True
        sems = {}
        counts = {}
        for o in self.all:
            if o.is_dma:
                key = o.semkey
            elif o.needed:
                key = ("eng", o.eng)
            else:
                continue
            if key not in sems:
                sems[key] = self.stack.enter_context(nc.semaphore("s%d" % len(sems)))
                counts[key] = 0
            o.ninc = 16 if o.is_dma else 1
            counts[key] += o.ninc
            o.sigval = (sems[key], counts[key])
            o.needed = True
        self.final_waits = [o for o in self.all if o.is_dma]
        return len(sems)

    def emit(self, block):
        engmap = {"pe": block.tensor, "act": block.scalar, "dve": block.vector,
                  "pool": block.gpsimd, "sp": block.sync}
        final_waits = self.final_waits

        def make(engname):
            ops = self.ops[engname]

            def body(e):
                waited = {}
                for o in ops:
                    need = {}
                    for d in o.deps:
                        if d.eng == "pe" and o.eng == "pe" and not d.is_dma:
                            continue
                        sem, val = d.sigval
                        k = id(sem)
                        if k not in need or need[k][1] < val:
                            need[k] = (sem, val)
                    for k, (sem, val) in need.items():
                        if waited.get(k, 0) >= val:
                            continue
                        waited[k] = val
                        e.wait_ge(sem, val)
                    inst = o.fn(e)
                    if o.needed:
                        sem, val = o.sigval
                        inst.then_inc(sem, o.ninc)
                if engname == "sp":
                    need = {}
                    for d in final_waits:
                        sem, val = d.sigval
                        k = id(sem)
                        if k not in need or need[k][1] < val:
                            need[k] = (sem, val)
                    for k, (sem, val) in need.items():
                        if waited.get(k, 0) >= val:
                            continue
                        waited[k] = val
                        e.wait_ge(sem, val)
            return body

        for engname in self.ENGS:
            engmap[engname](make(engname))


class Ring:
    def __init__(self, bufs):
        self.bufs = list(bufs)
        self.free = [None] * len(self.bufs)
        self.held = [False] * len(self.bufs)
        self.i = 0

    def get(self):
        idx = self.i % len(self.bufs)
        self.i += 1
        assert not self.held[idx], "ring slot re-acquired before its release was recorded"
        self.held[idx] = True
        return idx, self.bufs[idx], self.free[idx]

    def release(self, idx, ops):
        self.free[idx] = ops
        self.held[idx] = False


def AP(t, off, dims):
    return bass.AP(t, off, [list(d) for d in dims])


def build(debug=False, stop_after="C"):
    nc = bass.Bass("TRN2", target_bir_lowering=False)
    KS = "ExternalOutput" if debug else "Internal"

    def dram(name, shape, dt, kind):
        return nc.dram_tensor(name, shape, dt, kind=kind)

    x_d = dram("x", [S, D], F32, "ExternalInput")
    pos_d = dram("pos", [128, NT], I32, "ExternalInput")
    win_d = dram("w_in", [D, IPW], F32, "ExternalInput")
    wout_d = dram("w_out", [D, D], F32, "ExternalInput")
    wfi_d = dram("w_fi", [D, 2 * DFF], F32, "ExternalInput")
    wfo_d = dram("w_fo", [DFF, D], F32, "ExternalInput")
    pp_d = dram("pp", [128, PPW], F32, "ExternalInput")
    bc_d = dram("bc", [1, 4 * D], F32, "ExternalInput")
    out_d = dram("out", [S, D], F32, "ExternalOutput")
    QTd = dram("QTd", [6, 128, S], BF16, KS)
    KTd = dram("KTd", [6, 128, S], BF16, KS)
    Vd = dram("Vd", [S, 12 * 128], BF16, KS)
    mixTd = dram("mixTd", [8, 128, S], BF16, KS)
    x1d = dram("x1d", [S, D], F32, KS)
    wfis_d = dram("wfis", [NFC, 128, 8 * 256], BF16, "Internal")

    with contextlib.ExitStack() as st:
        def sb(name, shape, dt):
            return st.enter_context(nc.sbuf_tensor("s_" + name, shape, dt))

        P = Prog(nc, st)
        ps = st.enter_context(nc.psum_tensor("ps", [128, 4096], F32))
        psb = ps.bitcast(BF16)

        def bank(b, n=512, off=0):
            return ps[:, b * 512 + off: b * 512 + off + n]

        ident = sb("ident", [128, 128], BF16)
        onesf = sb("onesf", [128, 128], F32)
        pp = sb("pp", [128, PPW], F32)
        bcv = sb("bcv", [128, 4 * D], F32)
        cs = sb("cs", [128, NT, 32], F32)
        mbase = sb("mbase", [128, 256], BF16)
        mstd = sb("mstd", [128, 1024], BF16)
        m16 = sb("m16", [128, 384], BF16)
        wout_b = sb("wout_b", [128, 8, D], BF16)
        epsT = sb("epsT", [128, 1], F32)
        mhalf = sb("mhalf", [128, 1], F32)
        stU = contextlib.ExitStack()
        uTb = stU.enter_context(nc.sbuf_tensor("u_uTb", [128, 2, S + 32], BF16))
        dg = stU.enter_context(nc.sbuf_tensor("u_dg", [128, 2, 31, 128], BF16))

        final_waits = []
        stW = contextlib.ExitStack()
        win_b = stW.enter_context(nc.sbuf_tensor("u_win_b", [128, 8, IPW], BF16))
        ld_win = [P.dma(lambda e, dc=dc: e.dma_start(out=win_b[:, dc, :], in_=win_d.ap()[dc * 128:(dc + 1) * 128, :]),
                        semkey="win%d" % dc, eng="pool") for dc in range(8)]

        ld_pp = P.dma(lambda e: e.dma_start(out=pp[:], in_=pp_d.ap()), semkey="pp")
        ld_bc = P.dma(lambda e: e.dma_start(out=bcv[:], in_=bc_d.ap().partition_broadcast(128)), semkey="bc")
        c_id0 = P.op("pool", lambda e: e.memset(ident[:], 0.0))
        c_id = P.op("pool", lambda e: e.affine_select(out=ident[:], in_=ident[:], compare_op=ALU.not_equal, fill=1.0,
                                                      base=0, pattern=[[-1, 128]], channel_multiplier=1), [c_id0])
        c_ones = P.op("pool", lambda e: e.memset(onesf[:], 1.0 / 256.0))
        c_eps = P.op("pool", lambda e: e.memset(epsT[:], EPS))
        c_mhalf = P.op("pool", lambda e: e.memset(mhalf[:], -0.5))
        u_zero = P.op("pool", lambda e: e.memset(uTb[:], 0.0))
        k0 = P.op("pool", lambda e: e.memset(mbase[:], 1.0))
        k1 = P.op("pool", lambda e: e.affine_select(out=mbase[:], in_=mbase[:], compare_op=ALU.is_ge, fill=0.0,
                                                    base=0, pattern=[[1, 256]], channel_multiplier=-1), [k0])
        k2 = P.op("pool", lambda e: e.affine_select(out=mbase[:], in_=mbase[:], compare_op=ALU.is_ge, fill=0.0,
                                                    base=128, pattern=[[-1, 256]], channel_multiplier=1), [k1])
        std_pieces = [(192, 64), (64, 192), (0, 256), (0, 256), (0, 192), (0, 64)]
        mops = []
        col = 0
        for (j0, n) in std_pieces:
            mops.append(P.op("pool", lambda e, c=col, j=j0, n=n: e.tensor_copy(out=mstd[:, c:c + n], in_=mbase[:, j:j + n]), [k2]))
            col += n
        mops.append(P.op("pool", lambda e: e.tensor_copy(out=m16[:, 0:192], in_=mbase[:, 64:256]), [k2]))
        mops.append(P.op("pool", lambda e: e.tensor_copy(out=m16[:, 192:384], in_=mbase[:, 0:192]), [k2]))
        mask_ready = mops

        with contextlib.ExitStack() as st0:
            pos_i = st0.enter_context(nc.sbuf_tensor("r_pos_i", [128, NT], I32))
            pos_f = st0.enter_context(nc.sbuf_tensor("r_pos_f", [128, NT], F32))
            invf = st0.enter_context(nc.sbuf_tensor("r_invf", [128, 8], F32))
            th = st0.enter_context(nc.sbuf_tensor("r_th", [128, 2, NT, 8], F32))
            kf = st0.enter_context(nc.sbuf_tensor("r_kf", [128, 2, NT, 8], F32))
            ki = st0.enter_context(nc.sbuf_tensor("r_ki", [128, 2, NT, 8], I32))
            sn = st0.enter_context(nc.sbuf_tensor("r_sn", [128, 2, NT, 8], F32))
            ld_pos = P.dma(lambda e: e.dma_start(out=pos_i[:], in_=pos_d.ap()), semkey="pos")
            r0 = P.op("dve", lambda e: e.tensor_copy(out=pos_f[:], in_=pos_i[:]), [ld_pos])
            ri = [P.op("dve", lambda e, i=i: e.memset(invf[:, i:i + 1], INVF[i])) for i in range(8)]
            rt = []
            for t in range(NT):
                rt.append(P.op("dve", lambda e, t=t: e.tensor_scalar_mul(out=th[:, 0, t, :], in0=invf[:], scalar1=pos_f[:, t:t + 1]),
                               [r0, ri]))
            r1 = P.op("dve", lambda e: e.tensor_scalar_add(out=th[:, 1], in0=th[:, 0], scalar1=PI / 2), [rt])
            r2 = P.op("dve", lambda e: e.tensor_scalar_mul(out=kf[:], in0=th[:], scalar1=float(1.0 / (2 * np.pi))), [r1])
            r3 = P.op("dve", lambda e: e.tensor_copy(out=ki[:], in_=kf[:]), [r2])
            r4 = P.op("dve", lambda e: e.tensor_copy(out=kf[:], in_=ki[:]), [r3])
            r5 = P.op("dve", lambda e: e.scalar_tensor_tensor(out=th[:], in0=kf[:], scalar=-6.28125, in1=th[:],
                                                              op0=ALU.mult, op1=ALU.add), [r4])
            r6 = P.op("dve", lambda e: e.scalar_tensor_tensor(out=th[:], in0=kf[:], scalar=-0.0019353071795864769, in1=th[:],
                                                              op0=ALU.mult, op1=ALU.add), [r5])
            r7 = P.op("dve", lambda e: e.tensor_single_scalar(out=kf[:], in_=th[:], scalar=PI, op=ALU.is_gt), [r6])
            r8 = P.op("dve", lambda e: e.scalar_tensor_tensor(out=th[:], in0=kf[:], scalar=-2 * PI, in1=th[:],
                                                              op0=ALU.mult, op1=ALU.add), [r7])
            r9 = P.op("dve", lambda e: e.tensor_single_scalar(out=kf[:], in_=th[:], scalar=-PI, op=ALU.is_lt), [r8])
            r10 = P.op("dve", lambda e: e.scalar_tensor_tensor(out=th[:], in0=kf[:], scalar=2 * PI, in1=th[:],
                                                               op0=ALU.mult, op1=ALU.add), [r9])
            r11 = P.op("dve", lambda e: e.tensor_scalar(out=th[:], in0=th[:], scalar1=PI, scalar2=-PI,
                                                        op0=ALU.min, op1=ALU.max), [r10])
            r12 = P.op("act", lambda e: e.activation(out=sn[:], in_=th[:], func=AF.Sin), [r11])
            t0 = P.op("dve", lambda e: e.tensor_copy(out=cs[:, :, 0:8], in_=sn[:, 1]), [r12])
            t1 = P.op("dve", lambda e: e.tensor_copy(out=cs[:, :, 8:16], in_=sn[:, 1]), [r12])
            t2 = P.op("dve", lambda e: e.tensor_scalar_mul(out=cs[:, :, 16:24], in0=sn[:, 0], scalar1=-1.0), [r12])
            t3 = P.op("dve", lambda e: e.tensor_copy(out=cs[:, :, 24:32], in_=sn[:, 0]), [r12])
            cs_ready = [t0, t1, t2, t3]

            stA = st0.enter_context(contextlib.ExitStack())

            def sba(name, shape, dt):
                return stA.enter_context(nc.sbuf_tensor("a_" + name, shape, dt))

            xf_r = Ring([sba("xf%d" % i, [128, D], F32) for i in range(4)])
            xb_r = Ring([sba("xb%d" % i, [128, D], BF16) for i in range(2)])
            xT_r = Ring([sba("xT%d" % i, [128, 8, 512], BF16) for i in range(2)])
            qk_r = Ring([sba("qk%d" % i, [128, 1536], BF16) for i in range(2)])
            vt_r = Ring([sba("vt%d" % i, [128, 12, 128], BF16) for i in range(2)])
            qs_r = Ring([sba("qs%d" % i, [128, 12, 512], BF16) for i in range(2)])
            ra_r = Ring([sba("ra%d" % i, [128, 8, 16], F32) for i in range(2)])
            rb_r = Ring([sba("rb%d" % i, [128, 8, 16], F32) for i in range(2)])
            sg_r = Ring([sba("sg%d" % i, [128, 512], F32) for i in range(2)])
            u_written = []
            vones = []
            for i, vt in enumerate(vt_r.bufs):
                vones.append(P.op("pool", lambda e, vt=vt: e.memset(vt[:], 1.0)))
                vt_r.free[i] = [vones[-1]]

            pj_r = Ring([0, 1, 2])
            dg_ops = [P.op("pool", lambda e, cc=cc, j=j: e.tensor_scalar_mul(out=dg[:, cc, j, :], in0=ident[:], scalar1=pp[:, cc * 31 + j: cc * 31 + j + 1]),
                           [c_id, ld_pp]) for cc in range(2) for j in range(31)]
            xtp_r = Ring([3])
            qkt_r = Ring([4, 5])
            cv_r = Ring([6, 7])
            qtd_st = []
            vd_st = []

            def load_x(tt):
                i, buf, fr = xf_r.get()
                o = P.dma(lambda e, b=buf, tt=tt: e.dma_start(out=b[:], in_=x_d.ap()[tt * 128:(tt + 1) * 128, :]),
                          fr, semkey="xf%d" % i)
                return i, buf, o

            pend_x = [load_x(tt) for tt in range(4)]
            for sg in range(8):
                xi, xT, xT_free = xT_r.get()
                qsi, qs, qs_free = qs_r.get()
                xT_done = []
                for k in range(4):
                    tt = sg * 4 + k
                    fi, xf, ldx = pend_x.pop(0)
                    bi, xb, xb_free = xb_r.get()
                    cst = P.op("act", lambda e, a=xb, b=xf: e.activation(out=a[:], in_=b[:], func=AF.Copy), [ldx, xb_free])
                    xf_r.release(fi, [cst])
                    if tt + 4 < NT:
                        pend_x.append(load_x(tt + 4))
                    pi_, pbank, pfree = xtp_r.get()
                    trs = []
                    for dc in range(8):
                        trs.append(P.op("pe", lambda e, dc=dc, xb=xb, pb=pbank: e.transpose(
                            out=psb[:, pb * 1024 + dc * 128: pb * 1024 + (dc + 1) * 128], in_=xb[:, dc * 128:(dc + 1) * 128],
                            identity=ident[:]), [cst, c_id, pfree]))
                    xb_r.release(bi, [trs[-1]])
                    ev = P.op("dve", lambda e, xT=xT, pb=pbank, k=k: e.tensor_copy(
                        out=xT[:, :, k * 128:(k + 1) * 128],
                        in_=psb[:, pb * 1024: pb * 1024 + 1024].rearrange("p (c t) -> p c t", t=128)), [trs[-1], xT_free])
                    xtp_r.release(pi_, [ev])
                    xT_done.append(ev)
                xT_readers = []
                for k in range(4):
                    tt = sg * 4 + k
                    qi, qk, qk_free = qk_r.get()
                    vi, vt, vt_free = vt_r.get()
                    qk_w = []
                    vt_w = []
                    for cg in range(5):
                        ncol = 512 if cg < 4 else 256
                        bi_, b, bfree = pj_r.get()
                        mm = None
                        for dc in range(8):
                            mm = P.op("pe", lambda e, b=b, dc=dc, k=k, cg=cg, ncol=ncol, xT=xT: e.matmul(
                                bank(b, ncol), lhsT=xT[:, dc, k * 128:(k + 1) * 128], rhs=win_b[:, dc, cg * 512: cg * 512 + ncol],
                                start=(dc == 0), stop=(dc == 7)), [xT_done[k], ld_win[dc], bfree])
                        xT_readers.append(mm)
                        if cg < 3:
                            cp = P.op("act", lambda e, b=b, qk=qk, cg=cg: e.activation(
                                out=qk[:, cg * 512:(cg + 1) * 512], in_=bank(b), func=AF.Copy), [mm, qk_free])
                            ai, ra, ra_free = ra_r.get()
                            bi2, rb, rb_free = rb_r.get()
                            pv3 = lambda b, lo, n: AP(ps, b * 512 + lo, [[4096, 128], [64, 8], [1, n]])
                            csb = lambda off, n, tt=tt: AP(cs, tt * 32 + off, [[NT * 32, 128], [0, 8], [1, n]])
                            o1 = P.op("dve", lambda e, b=b, ra=ra, pv3=pv3, csb=csb: e.tensor_tensor(
                                out=ra[:], in0=pv3(b, 0, 16), in1=csb(0, 16), op=ALU.mult), [mm, cs_ready, ra_free, cp])
                            o2 = P.op("dve", lambda e, b=b, rb=rb, pv3=pv3, csb=csb: e.tensor_tensor(
                                out=rb[:, :, 0:8], in0=pv3(b, 8, 8), in1=csb(16, 8), op=ALU.mult), [mm, cs_ready, rb_free, cp])
                            o3 = P.op("dve", lambda e, b=b, rb=rb, pv3=pv3, csb=csb: e.tensor_tensor(
                                out=rb[:, :, 8:16], in0=pv3(b, 0, 8), in1=csb(24, 8), op=ALU.mult), [mm, cs_ready, rb_free, cp])
                            qv = AP(qk, cg * 512, [[1536, 128], [64, 8], [1, 16]])
                            o4 = P.op("dve", lambda e, ra=ra, rb=rb, qv=qv: e.tensor_tensor(
                                out=qv, in0=ra[:], in1=rb[:], op=ALU.add), [o1, o2, o3, cp])
                            ra_r.release(ai, [o4])
                            rb_r.release(bi2, [o4])
                            pj_r.release(bi_, [cp, o1, o2, o3])
                            qk_w.append(o4)
                        else:
                            nh = 8 if cg == 3 else 4
                            h0 = 0 if cg == 3 else 8
                            src_e = AP(ps, b * 512, [[4096, 128], [128, nh // 2], [1, 64]])
                            src_o = AP(ps, b * 512 + 64, [[4096, 128], [128, nh // 2], [1, 64]])
                            dst_e = AP(vt, h0 * 128, [[1536, 128], [256, nh // 2], [1, 64]])
                            dst_o = AP(vt, h0 * 128 + 128 + 64, [[1536, 128], [256, nh // 2], [1, 64]])
                            ce = P.op("act", lambda e, s=src_e, d=dst_e: e.activation(out=d, in_=s, func=AF.Copy), [mm, vt_free])
                            co = P.op("dve", lambda e, s=src_o, d=dst_o: e.tensor_copy(out=d, in_=s), [mm, vt_free, ce])
                            pj_r.release(bi_, [ce, co])
                            vt_w += [ce, co]
                    vst = P.dma(lambda e, vt=vt, tt=tt: e.dma_start(
                        out=Vd.ap()[tt * 128:(tt + 1) * 128, :], in_=vt[:].rearrange("p h c -> p (h c)")), vt_w, semkey="vst%d" % vi)
                    vt_r.release(vi, [vst])
                    vd_st.append(vst)
                    qk_rd = []
                    for half in range(2):
                        ti, tb, tfree = qkt_r.get()
                        trs = []
                        for c6 in range(6):
                            c = half * 6 + c6
                            trs.append(P.op("pe", lambda e, tb=tb, c6=c6, c=c, qk=qk: e.transpose(
                                out=psb[:, tb * 1024 + c6 * 128: tb * 1024 + (c6 + 1) * 128], in_=qk[:, c * 128:(c + 1) * 128],
                                identity=ident[:]), [qk_w, tfree]))
                        eng = "act" if half == 0 else "dve"
                        if eng == "act":
                            ev = P.op("act", lambda e, tb=tb, half=half, qs=qs, k=k: e.activation(
                                out=qs[:, half * 6:(half + 1) * 6, k * 128:(k + 1) * 128],
                                in_=psb[:, tb * 1024: tb * 1024 + 768].rearrange("p (c t) -> p c t", t=128), func=AF.Copy),
                                [trs[-1], qs_free])
                        else:
                            ev = P.op("dve", lambda e, tb=tb, half=half, qs=qs, k=k: e.tensor_copy(
                                out=qs[:, half * 6:(half + 1) * 6, k * 128:(k + 1) * 128],
                                in_=psb[:, tb * 1024: tb * 1024 + 768].rearrange("p (c t) -> p c t", t=128)),
                                [trs[-1], qs_free])
                        qkt_r.release(ti, [ev])
                        qk_rd.append(trs[-1])
                        qtd_st.append(ev)
                    qk_r.release(qi, qk_rd)
                qs_evs = qtd_st[-8:]
                us_w = []
                for cc in range(2):
                    gi_, gb, gfree = cv_r.get()
                    mg = None
                    for dc in range(8):
                        mg = P.op("pe", lambda e, gb=gb, dc=dc, cc=cc, xT=xT: e.matmul(
                            bank(gb), lhsT=win_b[:, dc, 2560 + cc * 128: 2560 + (cc + 1) * 128], rhs=xT[:, dc, :],
                            start=(dc == 0), stop=(dc == 7)), [xT_done, ld_win[dc], gfree])
                    si, sgt, sg_free = sg_r.get()
                    sgo = P.op("act", lambda e, gb=gb, sgt=sgt, cc=cc: e.activation(
                        out=sgt[:], in_=bank(gb), func=AF.Sigmoid, bias=pp[:, 70 + cc:71 + cc], scale=1.0), [mg, sg_free, ld_pp])
                    cv_r.release(gi_, [sgo])
                    vi_, vb, vfree = cv_r.get()
                    mv = None
                    for dc in range(8):
                        mv = P.op("pe", lambda e, vb=vb, dc=dc, cc=cc, xT=xT: e.matmul(
                            bank(vb), lhsT=win_b[:, dc, 2304 + cc * 128: 2304 + (cc + 1) * 128], rhs=xT[:, dc, :],
                            start=(dc == 0), stop=(dc == 7)), [xT_done, ld_win[dc], vfree])
                    xT_readers += [mg, mv]
                    uo = P.op("dve", lambda e, vb=vb, sgt=sgt, cc=cc, sg=sg: e.scalar_tensor_tensor(
                        out=uTb[:, cc, 16 + sg * 512: 16 + (sg + 1) * 512], in0=bank(vb), scalar=pp[:, 68 + cc:69 + cc], in1=sgt[:],
                        op0=ALU.add, op1=ALU.mult), [mv, sgo, u_zero, ld_pp])
                    cv_r.release(vi_, [uo])
                    sg_r.release(si, [uo])
                    us_w.append(uo)
                xT_r.release(xi, xT_readers)
                st_ops = []
                st_ops.append(P.dma(lambda e, qs=qs, sg=sg: e.dma_start(
                    out=QTd.ap()[:, :, sg * 512:(sg + 1) * 512].rearrange("c p t -> p c t"), in_=qs[:, 0:6, :]),
                    qs_evs, semkey="qst%d" % qsi))
                st_ops.append(P.dma(lambda e, qs=qs, sg=sg: e.dma_start(
                    out=KTd.ap()[:, :, sg * 512:(sg + 1) * 512].rearrange("c p t -> p c t"), in_=qs[:, 6:12, :]),
                    qs_evs, semkey="qst%d" % qsi))
                qs_r.release(qsi, list(st_ops))
                final_waits += st_ops
                u_written += us_w
            final_waits += vd_st
            qk_stores = [o for o in final_waits if o.semkey[1].startswith("qst")]
            stA.close()
            P.barrier()
        stW.close()
        if stop_after == "A":
            return _finish(nc, P, final_waits)

        with contextlib.ExitStack() as stC:
            def sbc(name, shape, dt):
                return stC.enter_context(nc.sbuf_tensor("c_" + name, shape, dt))

            qz = [sbc("qz%d" % i, [128, S], BF16) for i in range(2)]
            qz_zero = [P.op("pool", lambda e: e.memset(qz[0][64:128, :], 0.0)), P.op("pool", lambda e: e.memset(qz[1][0:64, :], 0.0))]
            q_free = [None]
            kt_r = Ring([sbc("kt%d" % i, [128, S], BF16) for i in range(2)])
            v_r = Ring([sbc("v%d" % i, [128, 32, 256], BF16) for i in range(3)])
            acc = [sbc("acc%d" % i, [128, S], F32) for i in range(2)]
            acc_last = [None, None]
            pt_r = Ring([sbc("pt%d" % i, [128, 1024], BF16) for i in range(3)])
            ms_r = Ring([sbc("ms%d" % i, [128, S], BF16) for i in range(1)])
            dsh_r = Ring([sbc("dsh%d" % i, [128, 1024], F32) for i in range(2)])
            s_r = Ring([0, 2, 4])
            o_r = Ring([6, 7])

            MASK_MOD = 3
            LOOK = 2
            chunks = []
            for hp in range(6):
                for d in (1, 4, 16):
                    L = S // d
                    nTl = L // 128
                    for h2 in range(2):
                        for r in range(d):
                            for M0 in range(0, L, 512):
                                M1 = min(L, M0 + 512)
                                pieces = []
                                c0 = 64 if (d != 16 and M0 == 0) else 0
                                col = c0
                                for t in range(max(0, M0 // 128 - 1), min(nTl - 1, M1 // 128) + 1):
                                    qlo = max(M0, 128 * t - 64)
                                    qhi = min(M1, 128 * t + 192)
                                    if qhi <= qlo:
                                        continue
                                    pieces.append((t, qlo, qhi, col, qlo - (128 * t - 64)))
                                    col += qhi - qlo
                                chunks.append(dict(hp=hp, d=d, h2=h2, r=r, M0=M0, M1=M1, pieces=pieces, ncol=col - c0, c0=c0))

            hp_state = {}

            def load_hp(hp):
                ki_, kt, kfree = kt_r.get()
                lk = P.dma(lambda e, kt=kt, hp=hp: e.dma_start(out=kt[:], in_=KTd.ap()[hp]), [kfree, qk_stores], semkey="lk%d" % ki_)
                hp_state[hp] = dict(lq=None, ki=ki_, kt=kt, lk=lk, readers=[], v={})

            def load_q(hp):
                ops = [P.dma(lambda e, hp=hp: e.dma_start(out=qz[0][0:64, :], in_=QTd.ap()[hp, 0:64, :]), [q_free[0], qk_stores], semkey="lqA"),
                       P.dma(lambda e, hp=hp: e.dma_start(out=qz[1][64:128, :], in_=QTd.ap()[hp, 64:128, :]), [q_free[0], qk_stores], semkey="lqB")]
                hp_state[hp]["lq"] = ops

            def load_v(hp, d):
                vi, vb, vfree = v_r.get()
                ops = []
                vsrc = Vd.ap()[:, hp * 256:(hp + 1) * 256]
                if d == 1:
                    ops.append(P.dma(lambda e, vb=vb, vsrc=vsrc: e.dma_start(
                        out=vb[:], in_=vsrc.rearrange("(t p) c -> p t c", p=128)), [vfree, vd_st], semkey="lv%d" % vi))
                else:
                    nt = S // d // 128
                    v4 = vsrc.rearrange("(t p r) c -> r p t c", p=128, r=d)
                    for r in range(d):
                        ops.append(P.dma(lambda e, vb=vb, r=r, nt=nt, v4=v4: e.dma_start(
                            out=vb[:, r * nt:(r + 1) * nt, :], in_=v4[r]), [vfree, vd_st], semkey="lv%d" % vi))
                hp_state[hp]["v"][d] = dict(vi=vi, vb=vb, ld=ops, readers=[])

            ld_wout = [P.dma(lambda e, dc=dc: e.dma_start(out=wout_b[:, dc, :], in_=wout_d.ap()[dc * 128:(dc + 1) * 128, :]),
                             semkey="wout%d" % dc, eng="pool") for dc in range(8)]
            wfis_ops = []
            wfis_todo = []
            for c in range(NFC):
                for half in range(2):
                    col0 = half * DFF + c * 128
                    src = wfi_d.ap()[:, col0:col0 + 128].rearrange("(dc p) f -> p dc f", p=128)
                    dst = wfis_d.ap()[c].rearrange("p (dc f) -> p dc f", f=256)[:, :, half * 128:(half + 1) * 128]
                    wfis_todo.append((src, dst, c))

            def issue_wfis():
                if wfis_todo:
                    src, dst, c = wfis_todo.pop(0)
                    wfis_ops.append(P.dma(lambda e, s=src, d=dst: e.dma_start(out=d, in_=s), semkey="wfis%d" % (c % 4), eng="pool"))

            load_hp(0)
            load_v(0, 1)
            load_v(0, 4)
            load_v(0, 16)
            acc1b = acc[1].bitcast(BF16)
            cva_r = Ring([acc[0][:, i * 512:(i + 1) * 512] for i in range(4)])
            sq_r = Ring([acc[0][:, 2048 + i * 512: 2048 + (i + 1) * 512] for i in range(4)])
            m2 = acc[1][:, 0:512]
            rstd = acc[1][:, 512:1024]
            cn_r = Ring([acc[1][:, 1024 + i * 512: 1024 + (i + 1) * 512] for i in range(2)])
            co_r = Ring([acc1b[:, 4096 + i * 512: 4096 + (i + 1) * 512] for i in range(4)])
            cp_r = Ring([2, 3, 4, 5])
            prev_stat = None
            for tg in range(8):
                cvs = []
                sqs = []
                for cc in range(2):
                    bi_, cb_, cfree = cp_r.get()
                    mm = None
                    for j in range(31):
                        mm = P.op("pe", lambda e, cb_=cb_, cc=cc, j=j, tg=tg: e.matmul(
                            bank(cb_), lhsT=dg[:, cc, j, :], rhs=uTb[:, cc, tg * 512 + 1 + j: tg * 512 + 513 + j],
                            start=(j == 0), stop=(j == 30)), [dg_ops, u_written, cfree])
                    ai, cva, afree = cva_r.get()
                    cv = P.op("act", lambda e, cva=cva, cb_=cb_, cc=cc: e.activation(
                        out=cva[:], in_=bank(cb_), func=AF.Identity, bias=pp[:, 62 + cc:63 + cc], scale=1.0), [mm, afree, ld_pp])
                    cp_r.release(bi_, [cv])
                    si, sq, sfree = sq_r.get()
                    sqo = P.op("act", lambda e, sq=sq, cva=cva: e.activation(out=sq[:], in_=cva[:], func=AF.Square), [cv, sfree])
                    cvs.append((ai, cva, cv))
                    sqs.append((si, sq, sqo))
                mm_mean = None
                for cc in range(2):
                    mm_mean = P.op("pe", lambda e, cc=cc, t=cvs[cc][1]: e.matmul(bank(0), lhsT=onesf[:], rhs=t[:], start=(cc == 0), stop=(cc == 1)),
                                   [cvs[cc][2], c_ones, prev_stat])
                mm_msq = None
                for cc in range(2):
                    mm_msq = P.op("pe", lambda e, cc=cc, t=sqs[cc][1]: e.matmul(bank(1), lhsT=onesf[:], rhs=t[:], start=(cc == 0), stop=(cc == 1)),
                                  [sqs[cc][2], c_ones, prev_stat])
                for cc in range(2):
                    sq_r.release(sqs[cc][0], [mm_msq])
                s1 = P.op("act", lambda e: e.activation(out=m2[:], in_=bank(0), func=AF.Square), [mm_mean, prev_stat])
                s2 = P.op("dve", lambda e: e.tensor_tensor(out=rstd[:], in0=bank(1), in1=m2[:], op=ALU.subtract), [mm_msq, s1, prev_stat])
                s3 = P.op("act", lambda e: e.activation(out=rstd[:], in_=rstd[:], func=AF.Sqrt, bias=epsT[:], scale=1.0), [s2, c_eps])
                s4 = P.op("dve", lambda e: e.reciprocal(out=rstd[:], in_=rstd[:]), [s3])
                lastn = []
                for cc in range(2):
                    ni, cn, nfree = cn_r.get()
                    n1 = P.op("dve", lambda e, cn=cn, t=cvs[cc][1]: e.tensor_tensor(out=cn[:], in0=t[:], in1=bank(0), op=ALU.subtract),
                              [mm_mean, nfree, cvs[cc][2], s1])
                    cva_r.release(cvs[cc][0], [n1, mm_mean])
                    n2 = P.op("dve", lambda e, cn=cn: e.tensor_tensor(out=cn[:], in0=cn[:], in1=rstd[:], op=ALU.mult), [n1, s4])
                    oi, co, ofree = co_r.get()
                    n3 = P.op("act", lambda e, cn=cn, co=co, cc=cc: e.activation(
                        out=co[:], in_=cn[:], func=AF.Silu, bias=pp[:, 66 + cc:67 + cc], scale=pp[:, 64 + cc:65 + cc]), [n2, ofree])
                    cn_r.release(ni, [n3])
                    cst = P.dma(lambda e, co=co, cc=cc, tg=tg: e.dma_start(
                        out=mixTd.ap()[6 + cc, :, tg * 512:(tg + 1) * 512], in_=co[:]), [n3], semkey="cost%d" % oi)
                    co_r.release(oi, [cst])
                    final_waits.append(cst)
                    lastn += [n1, n2]
                prev_stat = lastn + [s4]
            conv_stores = [o for o in final_waits if o.semkey[1].startswith("cost")]
            P.barrier()
            if stop_after == "A2":
                return _finish(nc, P, final_waits)
            mix_st = {}
            n_ch = len(chunks)
            stage = [None] * n_ch

            def emit_S(ci):
                ch = chunks[ci]
                hs = hp_state[ch["hp"]]
                si, sb_, sfree = s_r.get()
                d, r, h2 = ch["d"], ch["r"], ch["h2"]
                if hs["lq"] is None:
                    if ch["hp"] > 0:
                        q_free[0] = list(hp_state[ch["hp"] - 1]["readers"])
                    load_q(ch["hp"])
                mm = None
                for (t, qlo, qhi, col, j0) in ch["pieces"]:
                    n = qhi - qlo
                    kv = AP(hs["kt"], r + d * 128 * t, [[S, 128], [d, 128]])
                    qv = AP(qz[h2], r + d * qlo, [[S, 128], [d, n]])
                    mm = P.op("pe", lambda e, sb_=sb_, col=col, n=n, kv=kv, qv=qv: e.matmul(
                        ps[:, sb_ * 512 + col: sb_ * 512 + col + n], lhsT=kv, rhs=qv, start=True, stop=True),
                        [hs["lq"], hs["lk"], sfree, qz_zero])
                hs["readers"].append(mm)
                stage[ci] = dict(si=si, sb=sb_, smm=mm)

            def emit_em(ci):
                ch = chunks[ci]
                stg = stage[ci]
                d = ch["d"]
                ncol = ch["ncol"]
                sb_ = stg["sb"]
                pi_, pt, pfree = pt_r.get()
                c0 = ch["c0"]
                ex = P.op("act", lambda e, pt=pt, sb_=sb_, ncol=ncol, c0=c0: e.activation(
                    out=pt[:, c0:c0 + ncol], in_=ps[:, sb_ * 512 + c0: sb_ * 512 + c0 + ncol], func=AF.Exp, scale=0.125), [stg["smm"], pfree])
                s_r.release(stg["si"], [ex])
                if d == 16:
                    mk = m16[:, 0:ncol]
                else:
                    first = ch["pieces"][0]
                    assert first[4] == (64 if c0 == 64 else 192), "mask layout"
                    mk = mstd[:, c0: c0 + ncol]
                mo = P.op(("pool" if (ci % 2) == 0 else "dve"), lambda e, pt=pt, mk=mk, ncol=ncol, c0=c0: e.tensor_tensor(
                    out=pt[:, c0:c0 + ncol], in0=pt[:, c0:c0 + ncol], in1=mk, op=ALU.mult), [ex, mask_ready])
                stg["pi"] = pi_
                stg["pt"] = pt
                stg["mo"] = mo

            def emit_pv(ci):
                ch = chunks[ci]
                stg = stage[ci]
                hs = hp_state[ch["hp"]]
                d, r, h2 = ch["d"], ch["r"], ch["h2"]
                vs = hs["v"][d]
                pt = stg["pt"]
                oi, ob, ofree = o_r.get()
                nT = (S // d) // 128
                pvm = None
                for k, (t, qlo, qhi, col, j0) in enumerate(ch["pieces"]):
                    n = qhi - qlo
                    vv = vs["vb"][:, r * nT + t, h2 * 128:(h2 + 1) * 128]
                    pvm = P.op("pe", lambda e, ob=ob, vv=vv, pt=pt, col=col, n=n, o0=qlo - ch["M0"], k=k: e.matmul(
                        ps[:, ob * 512 + o0: ob * 512 + o0 + n], lhsT=vv, rhs=pt[:, col:col + n],
                        start=(k == 0), stop=True, skip_group_check=True), [stg["mo"], vs["ld"], ofree])
                pt_r.release(stg["pi"], [pvm])
                vs["readers"].append(pvm)
                stg["oi"] = oi
                stg["ob"] = ob
                stg["pvm"] = pvm

            def emit_evac(ci):
                ch = chunks[ci]
                stg = stage[ci]
                d, r, h2 = ch["d"], ch["r"], ch["h2"]
                ob, pvm = stg["ob"], stg["pvm"]
                nq = ch["M1"] - ch["M0"]
                a = acc[h2]
                av = AP(a, r + d * ch["M0"], [[S, 128], [d, nq]])
                if d == 1 and (ci % 2) == 1:
                    ev = P.op("act", lambda e, av=av, ob=ob, nq=nq: e.activation(out=av, in_=ps[:, ob * 512: ob * 512 + nq], func=AF.Copy),
                              [pvm, acc_last[h2]])
                elif d == 1:
                    ev = P.op("dve", lambda e, av=av, ob=ob, nq=nq: e.tensor_copy(out=av, in_=ps[:, ob * 512: ob * 512 + nq]),
                              [pvm, acc_last[h2]])
                else:
                    ev = P.op("dve", lambda e, av=av, ob=ob, nq=nq: e.tensor_tensor(
                        out=av, in0=ps[:, ob * 512: ob * 512 + nq], in1=av, op=ALU.add), [pvm, acc_last[h2]])
                acc_last[h2] = ev
                o_r.release(stg["oi"], [ev])
                stage[ci] = None
                return ev

            def finish_head(hp, h2, last_ev):
                if hp not in mix_st:
                    mi, ms, mfree = ms_r.get()
                    mix_st[hp] = dict(mi=mi, ms=ms, mfree=mfree, w=[])
                m = mix_st[hp]
                num = slice(0, 64) if h2 == 0 else slice(64, 128)
                den = slice(64, 128) if h2 == 0 else slice(0, 64)
                a = acc[h2]
                last = last_ev
                for q in range(4):
                    di, dsh, dfree = dsh_r.get()
                    c0_ = P.op("act", lambda e, dsh=dsh, a=a, q=q, num=num, den=den: e.activation(
                        out=dsh[num, :], in_=a[den, q * 1024:(q + 1) * 1024], func=AF.Ln), [last_ev, dfree])
                    c1 = P.op("act", lambda e, dsh=dsh, num=num: e.activation(
                        out=dsh[num, :], in_=dsh[num, :], func=AF.Exp, scale=-1.0), [c0_])
                    c2 = P.op("dve", lambda e, dsh=dsh, a=a, q=q, num=num, ms=m["ms"]: e.tensor_tensor(
                        out=ms[num, q * 1024:(q + 1) * 1024], in0=a[num, q * 1024:(q + 1) * 1024], in1=dsh[num, :], op=ALU.mult),
                        [c1, m["mfree"]])
                    dsh_r.release(di, [c2])
                    m["w"].append(c2)
                    last = c2
                acc_last[h2] = last
                if h2 == 1:
                    stq = P.dma(lambda e, ms=m["ms"], hp=hp: e.dma_start(out=mixTd.ap()[hp], in_=ms[:]), m["w"], semkey="mst%d" % m["mi"])
                    ms_r.release(m["mi"], [stq])
                    final_waits.append(stq)

            def chunk_done(ci, ev):
                ch = chunks[ci]
                last_of_head = (ci + 1 == n_ch) or (chunks[ci + 1]["h2"] != ch["h2"]) or (chunks[ci + 1]["d"] != ch["d"])
                last_of_pat = (ci + 1 == n_ch) or (chunks[ci + 1]["d"] != ch["d"]) or (chunks[ci + 1]["hp"] != ch["hp"])
                if last_of_pat:
                    vs = hp_state[ch["hp"]]["v"][ch["d"]]
                    v_r.release(vs["vi"], list(vs["readers"]))
                    if ch["hp"] + 1 < 6:
                        if ch["d"] == 1:
                            load_hp(ch["hp"] + 1)
                        load_v(ch["hp"] + 1, ch["d"])
                if ch["d"] == 16 and last_of_head:
                    finish_head(ch["hp"], ch["h2"], ev)
                if ch["d"] == 16 and last_of_pat:
                    hs = hp_state[ch["hp"]]
                    kt_r.release(hs["ki"], list(hs["readers"]))

            for t in range(-2, n_ch + 1):
                if 0 <= t + 2 < n_ch:
                    emit_S(t + 2)
                if 0 <= t + 1 < n_ch:
                    emit_em(t + 1)
                if 0 <= t < n_ch:
                    emit_pv(t)
                if 0 <= t - 1 < n_ch:
                    ev = emit_evac(t - 1)
                    chunk_done(t - 1, ev)
                if t >= 0 and t % 6 == 0:
                    issue_wfis()
            while wfis_todo:
                issue_wfis()
            mix_stores = [o for o in final_waits if o.semkey[1].startswith("mst")]
            P.barrier()
        stU.close()
        if stop_after == "B":
            return _finish(nc, P, final_waits)

        with contextlib.ExitStack() as stD:
            def sbd(name, shape, dt):
                return stD.enter_context(nc.sbuf_tensor("d_" + name, shape, dt))

            wfo_b = sbd("wfo_b", [128, NFC, D], BF16)
            ld_wfo = []

            def issue_wfo(c):
                ld_wfo.append(P.dma(lambda e: e.dma_start(out=wfo_b[:, c, :], in_=wfo_d.ap()[c * 128:(c + 1) * 128, :]),
                                    semkey="wfo%d" % (c % 8), eng="pool"))
            mx_r = Ring([sbd("mx%d" % i, [128, 8, 128], BF16) for i in range(3)])
            STQ = "pool"
            xr_r = Ring([sbd("xr%d" % i, [128, D], F32) for i in range(1)])
            y_r = Ring([sbd("y%d" % i, [128, D], F32) for i in range(2)])
            x1f_r = Ring([sbd("x1f%d" % i, [128, D], F32) for i in range(2)])
            x1b_r = Ring([sbd("x1b%d" % i, [128, D], BF16) for i in range(2)])
            x1T_sl = [sbd("x1T%d" % i, [128, 8, 512], BF16) for i in range(3)]
            xh = [sbd("xh%d" % i, [128, 8, 2], BF16) for i in range(8)]
            aT = sbd("aT", [128, NFC, 512], BF16)
            gext_r = Ring([sbd("gext%d" % i, [128, 514], F32) for i in range(2)])
            tt_r = Ring([sbd("tt%d" % i, [128, 512], F32) for i in range(2)])
            ss_r = Ring([sbd("ss%d" % i, [128, 512], F32) for i in range(2)])
            us_r = Ring([sbd("us%d" % i, [128, 512], F32) for i in range(2)])
            wb_r = Ring([sbd("wb%d" % i, [128, 8, 256], BF16) for i in range(3)])
            x1r_r = Ring([sbd("x1r%d" % i, [128, D], F32) for i in range(2)])
            ot_r = Ring([sbd("ot%d" % i, [128, D], F32) for i in range(2)])
            st_r = Ring([sbd("stt%d" % i, [128, 8], F32) for i in range(4)])
            junk = sbd("junk", [128, D], BF16)
            junk_last = [None]
            xh_init = [P.op("pool", lambda e, t=t: e.memset(t[:], 0.0)) for t in xh]
            xh_w = [[xh_init[i]] for i in range(8)]
            mf_r = Ring([0, 1, 2])
            h_r = Ring([3, 4, 5, 6])
            tp_r = mf_r
            hl_r = Ring([7])
            aT_free = [None]
            out_stores = []
            x1_stores = {}

            def layer_norm_steps(banks, resid, resid_dep, gcol, bcol, extra_deps):
                stx = {}

                def step1():
                    yi, y, yfree = y_r.get()
                    si, stt, sfree = st_r.get()
                    a0a = P.op("dve", lambda e: e.scalar_tensor_tensor(
                        out=y[:, 0:512], in0=resid[:, 0:512], scalar=ALPHA, in1=bank(banks[0]),
                        op0=ALU.mult, op1=ALU.add, accum_out=stt[:, 0:1]), [resid_dep, yfree, extra_deps, sfree])
                    a0 = P.op("dve", lambda e: e.scalar_tensor_tensor(
                        out=y[:, 512:1024], in0=resid[:, 512:1024], scalar=ALPHA, in1=bank(banks[1]),
                        op0=ALU.mult, op1=ALU.add, accum_out=stt[:, 7:8]), [resid_dep, yfree, extra_deps, a0a])
                    a1 = P.op("dve", lambda e: e.tensor_tensor(out=stt[:, 0:1], in0=stt[:, 0:1], in1=stt[:, 7:8], op=ALU.add), [a0, a0a])
                    a2 = P.op("act", lambda e: e.activation(out=junk[:], in_=y[:], func=AF.Square, accum_out=stt[:, 1:2]), [a0, sfree, junk_last[0]])
                    junk_last[0] = a2
                    stx.update(yi=yi, y=y, si=si, stt=stt, a1=a1, a2=a2)
                    return (a0a, a0)

                def step2():
                    stt, a1, a2 = stx["stt"], stx["a1"], stx["a2"]
                    b0 = P.op("dve", lambda e: e.tensor_scalar_mul(out=stt[:, 2:3], in0=stt[:, 0:1], scalar1=1.0 / D), [a1])
                    b1 = P.op("dve", lambda e: e.tensor_tensor(out=stt[:, 3:4], in0=stt[:, 2:3], in1=stt[:, 2:3], op=ALU.mult), [b0])
                    b2 = P.op("dve", lambda e: e.scalar_tensor_tensor(out=stt[:, 4:5], in0=stt[:, 1:2], scalar=1.0 / D, in1=stt[:, 3:4],
                                                                      op0=ALU.mult, op1=ALU.subtract), [b1, a2])
                    b3a = P.op("dve", lambda e: e.tensor_scalar_add(out=stt[:, 5:6], in0=stt[:, 4:5], scalar1=EPS), [b2])
                    b3 = P.op("pool", lambda e: e.tensor_tensor(out=stt[:, 6:7], in0=stt[:, 5:6], in1=mhalf[:], op=ALU.pow), [b3a, c_mhalf])
                    stx["b3"] = b3

                def step3(out_t, out_free):
                    stt, y, a2, b3 = stx["stt"], stx["y"], stx["a2"], stx["b3"]
                    n0 = P.op("dve", lambda e: e.tensor_scalar(out=y[:], in0=y[:], scalar1=stt[:, 2:3], scalar2=stt[:, 6:7],
                                                               op0=ALU.subtract, op1=ALU.mult), [b3, a2])
                    n1 = P.op("dve", lambda e: e.tensor_tensor(out=y[:], in0=y[:], in1=bcv[:, gcol * D:(gcol + 1) * D], op=ALU.mult), [n0, ld_bc])
                    n2 = P.op("dve", lambda e: e.tensor_tensor(out=out_t[:], in0=y[:], in1=bcv[:, bcol * D:(bcol + 1) * D], op=ALU.add), [n1, out_free])
                    y_r.release(stx["yi"], [n2])
                    st_r.release(stx["si"], [n0])
                    return n2

                return step1, step2, step3

            def layer_norm(banks, resid, resid_dep, gcol, bcol, out_t, out_free, extra_deps):
                s1_, s2_, s3_ = layer_norm_steps(banks, resid, resid_dep, gcol, bcol, extra_deps)
                a0 = s1_()
                s2_()
                n2 = s3_(out_t, out_free)
                return a0, n2

            NSLOT = 3
            slot_free = [None] * NSLOT
            tile_ev = {}
            s1st = {}

            s1a = {}
            s1b = {}

            def s1_A1(j):
                mi, mx, mfree = mx_r.get()
                ldm = P.dma(lambda e: e.dma_start(out=mx[:], in_=mixTd.ap()[:, :, j * 128:(j + 1) * 128].rearrange("c p t -> p c t")),
                            [mfree, mix_stores, conv_stores], semkey="mx%d" % mi)
                ri_, xr, rfree = xr_r.get()
                ldx = P.dma(lambda e: e.dma_start(out=xr[:], in_=x_d.ap()[j * 128:(j + 1) * 128, :]), [rfree], semkey="xr%d" % ri_)
                bi_ = []
                b = []
                mm = None
                for half in range(2):
                    i_, b_, bfree = mf_r.get()
                    bi_.append(i_)
                    b.append(b_)
                    for c in range(8):
                        mm = P.op("pe", lambda e, half=half, c=c, b_=b_: e.matmul(
                            bank(b_), lhsT=mx[:, c, :], rhs=wout_b[:, c, half * 512:(half + 1) * 512],
                            start=(c == 0), stop=(c == 7)), [ldm, ld_wout, bfree])
                mx_r.release(mi, [mm])
                s1a[j] = (ri_, xr, ldx, bi_, b, mm)

            s1ln = {}

            def s1_A2a(j):
                ri_, xr, ldx, bi_, b, mm = s1a.pop(j)
                st1, st2, st3 = layer_norm_steps(b, xr, ldx, 0, 1, [mm])
                a0 = st1()
                mf_r.release(bi_[0], [a0[0]])
                mf_r.release(bi_[1], [a0[1]])
                xr_r.release(ri_, [a0[1]])
                s1ln[j] = (st2, st3)

            def s1_A2b(j):
                s1ln[j][0]()

            def s1_A2c(j):
                st3 = s1ln.pop(j)[1]
                fi, x1f, ffree = x1f_r.get()
                n2 = st3(x1f, ffree)
                x1s = P.dma(lambda e: e.dma_start(out=x1d.ap()[j * 128:(j + 1) * 128, :], in_=x1f[:]), [n2], semkey="x1s%d" % fi, eng=STQ)
                x1_stores[j] = x1s
                bi2, x1b, bbfree = x1b_r.get()
                cb = P.op("act", lambda e: e.activation(out=x1b[:], in_=x1f[:], func=AF.Copy), [n2, bbfree])
                x1f_r.release(fi, [x1s, cb])
                s1st[j] = (bi2, x1b, cb)

            def s1_A2(j):
                s1_A2a(j)
                s1_A2b(j)
                s1_A2c(j)

            def s1_B1(j):
                bi2, x1b, cb = s1st.pop(j)
                g, k = j // 4, j % 4
                x1T = x1T_sl[g % NSLOT]
                ti, tb, tfree = tp_r.get()
                trs = None
                for dc in range(8):
                    trs = P.op("pe", lambda e, dc=dc: e.transpose(
                        out=psb[:, tb * 1024 + dc * 128: tb * 1024 + (dc + 1) * 128], in_=x1b[:, dc * 128:(dc + 1) * 128],
                        identity=ident[:]), [cb, tfree])
                x1b_r.release(bi2, [trs])
                s1b[j] = (ti, tb, trs)

            def s1_B2(j):
                ti, tb, trs = s1b.pop(j)
                g, k = j // 4, j % 4
                x1T = x1T_sl[g % NSLOT]
                ev = P.op("dve", lambda e: e.tensor_copy(
                    out=x1T[:, :, k * 128:(k + 1) * 128],
                    in_=psb[:, tb * 1024: tb * 1024 + 1024].rearrange("p (c t) -> p c t", t=128)), [trs, slot_free[g % NSLOT]])
                tp_r.release(ti, [ev])
                tile_ev[j] = ev
                if k == 0 and g > 0:
                    xh_w[g - 1].append(P.op("pool", lambda e: e.tensor_copy(out=xh[g - 1][:, :, 1:2], in_=x1T[:, :, 0:1]), [ev, xh_w[g - 1]]))
                if k == 3 and g < 7:
                    xh_w[g + 1].append(P.op("pool", lambda e: e.tensor_copy(out=xh[g + 1][:, :, 0:1], in_=x1T[:, :, 511:512]), [ev, xh_w[g + 1]]))

            def stage2(g):
                x1T = x1T_sl[g % NSLOT]
                s1evs = [tile_ev[4 * g + k] for k in range(4)]
                hooks = {}
                for i, j in enumerate(range(4 * g + 5, 4 * g + 9)):
                    if j < NT:
                        for off, fn in ((0, s1_A1), (1, s1_A2a), (2, s1_A2b), (3, s1_A2c), (5, s1_B1), (6, s1_B2)):
                            hooks.setdefault(5 * i + off, []).append((fn, j))
                x1T_rd = []
                a_w = []
                for c in range(NFC):
                    wi, wb, wfree = wb_r.get()
                    ldw = P.dma(lambda e, wb=wb, c=c: e.dma_start(out=wb[:].rearrange("p a b -> p (a b)"), in_=wfis_d.ap()[c]),
                                [wfree, wfis_ops], semkey="wb%d" % wi)
                    hi, hb, hfree = h_r.get()
                    ui_, ub_, ufree_ = h_r.get()
                    li, lb, lfree = hl_r.get()
                    mg = None
                    for dc in range(8):
                        mg = P.op("pe", lambda e, dc=dc, wb=wb, hb=hb: e.matmul(
                            bank(hb), lhsT=wb[:, dc, 0:128], rhs=x1T[:, dc, :], start=(dc == 0), stop=(dc == 7)),
                            [ldw, s1evs, hfree])
                    mu = None
                    for dc in range(8):
                        mu = P.op("pe", lambda e, dc=dc, wb=wb, ub_=ub_: e.matmul(
                            bank(ub_), lhsT=wb[:, dc, 128:256], rhs=x1T[:, dc, :], start=(dc == 0), stop=(dc == 7)), [ufree_])
                    mh = None
                    for dc in range(8):
                        mh = P.op("pe", lambda e, dc=dc, wb=wb, lb=lb: e.matmul(
                            ps[:, lb * 512: lb * 512 + 2], lhsT=wb[:, dc, 0:128], rhs=xh[g][:, dc, :], start=(dc == 0), stop=(dc == 7)),
                            [xh_w[g], lfree])
                    wb_r.release(wi, [mh])
                    x1T_rd.append(mu)
                    gi, gext, gfree = gext_r.get()
                    ti, tt_, tfree = tt_r.get()
                    si, ss, sfree = ss_r.get()
                    e0 = P.op("act", lambda e, gext=gext, hb=hb: e.activation(out=gext[:, 1:513], in_=bank(hb), func=AF.Copy), [mg, gfree])
                    e1 = P.op("dve", lambda e, gext=gext, lb=lb: e.tensor_copy(
                        out=AP(gext, 0, [[514, 128], [513, 2]]), in_=ps[:, lb * 512: lb * 512 + 2]), [mh, gfree])
                    hl_r.release(li, [e1])
                    e2 = P.op("act", lambda e, tt_=tt_, hb=hb, c=c: e.activation(
                        out=tt_[:], in_=bank(hb), func=AF.Identity, scale=pp[:, 72 + c * 3 + 1: 72 + c * 3 + 2],
                        bias=pp[:, 138 + c:139 + c]), [mg, tfree, ld_pp])
                    e3 = P.op("dve", lambda e, tt_=tt_, gext=gext, c=c: e.scalar_tensor_tensor(
                        out=tt_[:], in0=gext[:, 0:512], scalar=pp[:, 72 + c * 3: 72 + c * 3 + 1], in1=tt_[:], op0=ALU.mult, op1=ALU.add),
                        [e0, e1, e2])
                    e4 = P.op("dve", lambda e, tt_=tt_, gext=gext, c=c: e.scalar_tensor_tensor(
                        out=tt_[:], in0=gext[:, 2:514], scalar=pp[:, 72 + c * 3 + 2: 72 + c * 3 + 3], in1=tt_[:], op0=ALU.mult, op1=ALU.add),
                        [e3])
                    gext_r.release(gi, [e4])
                    e5 = P.op("act", lambda e, ss=ss, tt_=tt_: e.activation(out=ss[:], in_=tt_[:], func=AF.Silu), [e4, sfree])
                    tt_r.release(ti, [e5])
                    vi_, us_, vfree_ = us_r.get()
                    eu = P.op("act", lambda e, us_=us_, ub_=ub_: e.activation(out=us_[:], in_=bank(ub_), func=AF.Copy), [mu, vfree_])
                    e6 = P.op("pool", lambda e, ss=ss, us_=us_, c=c: e.tensor_tensor(
                        out=aT[:, c, :], in0=us_[:], in1=ss[:], op=ALU.mult), [e5, eu, aT_free[0]])
                    ss_r.release(si, [e6])
                    us_r.release(vi_, [e6])
                    h_r.release(hi, [e0, e2])
                    h_r.release(ui_, [eu])
                    a_w.append(e6)
                    for fn, jj in hooks.get(c, []):
                        fn(jj)
                    if g == 0:
                        issue_wfo(c)
                slot_free[g % NSLOT] = x1T_rd
                last_mm = None
                pend = None

                def ln2(args):
                    tt, ri_, x1r, ldx, bi_, b, mm = args
                    oi, ot, ofree = ot_r.get()
                    a0, n2 = layer_norm(b, x1r, ldx, 2, 3, ot, ofree, [mm])
                    mf_r.release(bi_[0], [a0[0]])
                    mf_r.release(bi_[1], [a0[1]])
                    x1r_r.release(ri_, [a0[1]])
                    ost = P.dma(lambda e: e.dma_start(out=out_d.ap()[tt * 128:(tt + 1) * 128, :], in_=ot[:]), [n2], semkey="ost%d" % oi, eng=STQ)
                    ot_r.release(oi, [ost])
                    out_stores.append(ost)

                for k in range(4):
                    tt = g * 4 + k
                    ri_, x1r, rfree = x1r_r.get()
                    ldx = P.dma(lambda e, x1r=x1r, tt=tt: e.dma_start(out=x1r[:], in_=x1d.ap()[tt * 128:(tt + 1) * 128, :]),
                                [rfree, x1_stores[tt]], semkey="x1r%d" % ri_)
                    bi_ = []
                    b = []
                    mm = None
                    for half in range(2):
                        if half == 1 and pend is not None:
                            ln2(pend)
                            pend = None
                        i_, b_, bfree = mf_r.get()
                        bi_.append(i_)
                        b.append(b_)
                        for c in range(NFC):
                            mm = P.op("pe", lambda e, half=half, c=c, k=k, b_=b_: e.matmul(
                                bank(b_), lhsT=aT[:, c, k * 128:(k + 1) * 128], rhs=wfo_b[:, c, half * 512:(half + 1) * 512],
                                start=(c == 0), stop=(c == NFC - 1)), [a_w[c], ld_wfo, bfree])
                    last_mm = mm
                    pend = (tt, ri_, x1r, ldx, bi_, b, mm)
                ln2(pend)
                aT_free[0] = [last_mm]

            steps = (s1_A1, s1_A2a, s1_A2b, s1_A2c, s1_B1, s1_B2)
            for sl in range(0, 2 * 4 + 6):
                for j in range(5):
                    k = sl - 2 * j
                    if 0 <= k < 6:
                        steps[k](j)
            for g in range(8):
                stage2(g)
            final_waits += out_stores
        return _finish(nc, P, final_waits)


def _finish(nc, P, final_waits):
    nsem = P.finalize(final_waits)
    with nc.Block() as block:
        P.emit(block)
    return nc


_NC_CACHE = {}


def _pack_small(inp):
    pp = np.zeros((128, PPW), np.float32)
    cw = np.asarray(inp["conv_w"], np.float32)[0]
    for cc in range(2):
        pp[:, cc * 31:(cc + 1) * 31] = cw[:, cc * 128:(cc + 1) * 128].T
    for k, name in ((62, "conv_b"), (64, "conv_ln_g"), (66, "conv_ln_b")):
        v = np.asarray(inp[name], np.float32)[0]
        pp[:, k:k + 2] = v.reshape(2, 128).T
    bg = np.asarray(inp["b_glu"], np.float32)[0]
    pp[:, 68:70] = bg[:256].reshape(2, 128).T
    pp[:, 70:72] = bg[256:].reshape(2, 128).T
    fw = np.asarray(inp["ffn_conv_w"], np.float32)[0]
    pp[:, 72:138] = fw.T.reshape(NFC, 128, 3).transpose(1, 0, 2).reshape(128, NFC * 3)
    fb = np.asarray(inp["ffn_conv_b"], np.float32)[0]
    pp[:, 138:160] = fb.reshape(NFC, 128).T
    bc = np.concatenate([np.asarray(inp[n], np.float32)[0] for n in ("ln1_g", "ln1_b", "ln2_g", "ln2_b")])[None, :]
    return pp, np.ascontiguousarray(bc)


def make_in_maps(inp, n_cores=8):
    pp, bc = _pack_small(inp)
    x = np.asarray(inp["x"], np.float32)
    pos = np.asarray(inp["positions"], np.int32)
    shared = {
        "w_in": np.ascontiguousarray(np.asarray(inp["w_in"], np.float32)[0]),
        "w_out": np.ascontiguousarray(np.asarray(inp["w_out"], np.float32)[0]),
        "w_fi": np.ascontiguousarray(np.asarray(inp["w_ffn_in"], np.float32)[0]),
        "w_fo": np.ascontiguousarray(np.asarray(inp["w_ffn_out"], np.float32)[0]),
        "pp": pp, "bc": bc,
    }
    maps = []
    for b in range(n_cores):
        m = dict(shared)
        m["x"] = np.ascontiguousarray(x[b])
        m["pos"] = np.ascontiguousarray(pos[b].reshape(NT, 128).T)
        maps.append(m)
    return maps


def kernel(**inputs):
    if "nc" not in _NC_CACHE:
        _NC_CACHE["nc"] = build()
    nc = _NC_CACHE["nc"]
    maps = make_in_maps(inputs, 8)
    res = run_bass_kernel_spmd(nc, maps, core_ids=list(range(8)))
    return np.stack([np.asarray(r["out"], np.float32) for r in res.results], axis=0)
```
